# Optimizing a Trainium2 kernel written in Bass

```python
import math
import jax, jax.numpy as jnp
from jax import lax
import numpy as np

D_MODEL = 1024
BATCH = 8
SEQ = 4096
DEPTH = 1

SSM_D_INNER = D_MODEL
SSM_HEAD_DIM = 64
SSM_HEADS = SSM_D_INNER // SSM_HEAD_DIM
SSM_GROUPS = 4
SSM_STATE = 128
SSM_CONV = 4
SSM_CHUNK = 128
SSM_CONV_DIM = SSM_D_INNER + 2 * SSM_GROUPS * SSM_STATE
ATT_D = D_MODEL
ATT_HEAD_DIM = 64
ATT_HEADS = ATT_D // ATT_HEAD_DIM
ATT_BLOCK = 128
RMS_EPS = 1e-6
IN_SPLITS = (SSM_D_INNER, SSM_CONV_DIM, SSM_HEADS, ATT_D, ATT_D, ATT_D, ATT_HEADS, ATT_D, D_MODEL, D_MODEL)
IN_COLS = SSM_D_INNER + SSM_CONV_DIM + SSM_HEADS + 4 * ATT_D + ATT_HEADS + 2 * D_MODEL

kernel_name = 'hybrid_ssd_fox_gated_merge'


def _offsets(sizes):
    out, acc = [], 0
    for s in sizes[:-1]:
        acc += s
        out.append(acc)
    return out


def _rmsnorm(x, g):
    xf = x.astype(jnp.float32)
    y = xf * lax.rsqrt(jnp.mean(xf * xf, axis=-1, keepdims=True) + RMS_EPS) * g.astype(jnp.float32)
    return y.astype(x.dtype)


def _causal_dwconv(u, w, b):
    y = lax.conv_general_dilated(u, w[:, None, :].astype(u.dtype), window_strides=(1,),
                                 padding=((SSM_CONV - 1, 0),),
                                 dimension_numbers=('NWC', 'WIO', 'NWC'),
                                 feature_group_count=u.shape[-1])
    return y + b.astype(u.dtype)


def _ssd_chunked(xs, dt, a, b_mat, c_mat, d_skip):
    bsz, seqlen, n_heads, p = xs.shape
    g, n = b_mat.shape[2], b_mat.shape[3]
    r = n_heads // g
    nc = seqlen // SSM_CHUNK
    q = SSM_CHUNK
    xdt = (xs * dt[..., None]).reshape(bsz, nc, q, g, r, p)
    bc = b_mat.reshape(bsz, nc, q, g, n)
    cc = c_mat.reshape(bsz, nc, q, g, n)
    da_cs = jnp.cumsum((dt * a).reshape(bsz, nc, q, g, r), axis=2)
    causal = jnp.tril(jnp.ones((q, q), dtype=bool))
    seg = da_cs[:, :, :, None] - da_cs[:, :, None, :]
    decay = jnp.exp(jnp.where(causal[None, None, :, :, None, None], seg, -jnp.inf))
    cb = jnp.einsum('bclgn,bcsgn->bclsg', cc, bc)
    y_diag = jnp.einsum('bclsgr,bcsgrp->bclgrp', cb[..., None] * decay, xdt)
    to_end = jnp.exp(da_cs[:, :, -1:] - da_cs)
    states = jnp.einsum('bcsgn,bcsgrp->bcgrpn', bc, xdt * to_end[..., None])
    chunk_decay = jnp.exp(da_cs[:, :, -1])

    def step(carry, inp):
        st, dec = inp
        return carry * dec[..., None, None] + st, carry

    init = jnp.zeros((bsz, g, r, p, n), states.dtype)
    _, prev = lax.scan(step, init, (jnp.moveaxis(states, 1, 0), jnp.moveaxis(chunk_decay, 1, 0)))
    prev = jnp.moveaxis(prev, 0, 1)
    y_off = jnp.einsum('bclgn,bcgrpn->bclgrp', cc, prev) * jnp.exp(da_cs)[..., None]
    y = (y_diag + y_off).reshape(bsz, seqlen, n_heads, p)
    return y + xs * d_skip[:, None]


def _forgetting_attention(q, k, v, log_f):
    seqlen = q.shape[2]
    scale = ATT_HEAD_DIM ** -0.5
    fcum = jnp.cumsum(log_f, axis=-1)
    outs = []
    for blk in range(seqlen // ATT_BLOCK):
        q0, q1 = blk * ATT_BLOCK, (blk + 1) * ATT_BLOCK
        s = jnp.einsum('bhqd,bhkd->bhqk', q[:, :, q0:q1], k[:, :, :q1]).astype(jnp.float32) * scale
        s = s + fcum[:, :, q0:q1, None] - fcum[:, :, None, :q1]
        qpos = q0 + jnp.arange(ATT_BLOCK)
        kpos = jnp.arange(q1)
        s = jnp.where(kpos[None, :] <= qpos[:, None], s, -jnp.inf)
        prob = jax.nn.softmax(s, axis=-1)
        outs.append(jnp.einsum('bhqk,bhkd->bhqd', prob.astype(v.dtype), v[:, :, :q1]))
    return jnp.concatenate(outs, axis=2)


def _hybrid_layer(x, pre_g, w_in, conv_w, conv_b, dt_bias, a_log, d_skip, ssm_norm_g,
                  fgate_b, w_branch_ssm, w_branch_att, w_out, post_g):
    bsz, seqlen, _ = x.shape
    f32 = jnp.float32
    h = _rmsnorm(x, pre_g)
    proj = jnp.einsum('bld,de->ble', h, w_in)
    z_ssm, xbc, dt_raw, q, k, v, f_raw, z_att, g_ssm, g_att = jnp.split(proj, _offsets(IN_SPLITS), axis=-1)

    xbc = jax.nn.silu(_causal_dwconv(xbc, conv_w, conv_b))
    gn = SSM_GROUPS * SSM_STATE
    xs, b_ssm, c_ssm = jnp.split(xbc, [SSM_D_INNER, SSM_D_INNER + gn], axis=-1)
    dt = jax.nn.softplus(dt_raw.astype(f32) + dt_bias.astype(f32))
    a = -jnp.exp(a_log.astype(f32))
    y = _ssd_chunked(xs.astype(f32).reshape(bsz, seqlen, SSM_HEADS, SSM_HEAD_DIM), dt, a,
                     b_ssm.astype(f32).reshape(bsz, seqlen, SSM_GROUPS, SSM_STATE),
                     c_ssm.astype(f32).reshape(bsz, seqlen, SSM_GROUPS, SSM_STATE),
                     d_skip.astype(f32))
    yg = (y.reshape(bsz, seqlen, SSM_D_INNER) * jax.nn.silu(z_ssm.astype(f32)))
    yg = yg.reshape(bsz, seqlen, SSM_GROUPS, SSM_D_INNER // SSM_GROUPS)
    yg = yg * lax.rsqrt(jnp.mean(yg * yg, axis=-1, keepdims=True) + RMS_EPS)
    y_ssm = (yg.reshape(bsz, seqlen, SSM_D_INNER) * ssm_norm_g.astype(f32)).astype(x.dtype)

    def heads(t):
        return t.reshape(bsz, seqlen, ATT_HEADS, ATT_HEAD_DIM).transpose(0, 2, 1, 3)
    log_f = jax.nn.log_sigmoid(f_raw.astype(f32) + fgate_b.astype(f32)).transpose(0, 2, 1)
    o = _forgetting_attention(heads(q), heads(k), heads(v), log_f)
    o = o.transpose(0, 2, 1, 3).reshape(bsz, seqlen, ATT_D)
    y_att = (o.astype(f32) * jax.nn.silu(z_att.astype(f32))).astype(x.dtype)

    p_ssm = jnp.einsum('ble,ed->bld', y_ssm, w_branch_ssm)
    p_att = jnp.einsum('ble,ed->bld', y_att, w_branch_att)
    merged = jax.nn.sigmoid(g_ssm) * p_ssm + jax.nn.sigmoid(g_att) * p_att
    out = jnp.einsum('bld,de->ble', merged, w_out)
    return x + _rmsnorm(out, post_g)


def setup_inputs(seed: int = 0) -> dict:
    key = jax.random.key(seed)
    ks = jax.random.split(key, 16)
    f32 = jnp.float32
    nl = DEPTH
    nrm = jax.random.normal
    x = nrm(ks[0], (BATCH, SEQ, D_MODEL), f32)
    pre_norm_g = 1.0 + 0.05 * nrm(ks[1], (nl, D_MODEL), f32)
    w_in = nrm(ks[2], (nl, D_MODEL, IN_COLS), f32) * D_MODEL ** -0.5
    conv_w = nrm(ks[3], (nl, SSM_CONV, SSM_CONV_DIM), f32) * SSM_CONV ** -0.5
    conv_b = 0.02 * nrm(ks[4], (nl, SSM_CONV_DIM), f32)
    dt0 = jnp.exp(jax.random.uniform(ks[5], (nl, SSM_HEADS), f32, minval=math.log(1e-3), maxval=math.log(1e-1)))
    dt_bias = dt0 + jnp.log(-jnp.expm1(-dt0))
    a_log = jnp.log(jax.random.uniform(ks[6], (nl, SSM_HEADS), f32, minval=1.0, maxval=16.0))
    d_skip = 1.0 + 0.1 * nrm(ks[7], (nl, SSM_HEADS), f32)
    ssm_norm_g = 1.0 + 0.05 * nrm(ks[8], (nl, SSM_D_INNER), f32)
    fgate_b = jax.random.uniform(ks[9], (nl, ATT_HEADS), f32, minval=1.0, maxval=4.0)
    w_branch_ssm = nrm(ks[10], (nl, SSM_D_INNER, D_MODEL), f32) * SSM_D_INNER ** -0.5
    w_branch_att = nrm(ks[11], (nl, ATT_D, D_MODEL), f32) * ATT_D ** -0.5
    w_out = nrm(ks[12], (nl, D_MODEL, D_MODEL), f32) * D_MODEL ** -0.5
    post_norm_g = 1.0 + 0.05 * nrm(ks[13], (nl, D_MODEL), f32)
    return {'x': x, 'pre_norm_g': pre_norm_g, 'w_in': w_in, 'conv_w': conv_w, 'conv_b': conv_b,
            'dt_bias': dt_bias, 'a_log': a_log, 'd_skip': d_skip, 'ssm_norm_g': ssm_norm_g,
            'fgate_b': fgate_b, 'w_branch_ssm': w_branch_ssm, 'w_branch_att': w_branch_att,
            'w_out': w_out, 'post_norm_g': post_norm_g}


def reference(x, pre_norm_g, w_in, conv_w, conv_b, dt_bias, a_log, d_skip, ssm_norm_g,
              fgate_b, w_branch_ssm, w_branch_att, w_out, post_norm_g):
    for i in range(DEPTH):
        x = _hybrid_layer(x, pre_norm_g[i], w_in[i], conv_w[i], conv_b[i], dt_bias[i], a_log[i],
                          d_skip[i], ssm_norm_g[i], fgate_b[i], w_branch_ssm[i], w_branch_att[i],
                          w_out[i], post_norm_g[i])
    return x
```

```python
import numpy as np
from contextlib import ExitStack
import concourse.bass as bass
import concourse.mybir as mybir
from concourse.bass_utils import run_bass_kernel_spmd

F32 = mybir.dt.float32
BF16 = mybir.dt.bfloat16
ALU = mybir.AluOpType
AF = mybir.ActivationFunctionType

ENGS = ("pe", "act", "dve", "pool", "sp")
NDMASEM = 8
NPP = 6
PSUM_NAMES = {"pt", "pg", "pp", "pst", "pO", "ppx", "pseg", "pcb", "pyd", "pyo", "ptr", "pps", "ppa", "pgs", "pga", "pout"}

L = 4096
D = 1024
NTB = 32
NSLAB = 8
O_ZS, O_XBC, O_DT, O_Q, O_K, O_V, O_F, O_ZA, O_GS, O_GA = 0, 1024, 3072, 3088, 4112, 5136, 6160, 6176, 7200, 8224
NCOLS = 9248
EPS = 1e-6


class _Op:
    __slots__ = ("eng", "fn", "deps", "is_dma", "sig", "dma_slot", "dma_val", "idx", "has_dep")

    def __init__(self, eng, fn, is_dma):
        self.eng = eng
        self.fn = fn
        self.deps = []
        self.is_dma = is_dma
        self.sig = None
        self.has_dep = False
        self.dma_slot = None
        self.dma_val = None
        self.idx = None


class Sched:
    def __init__(self, nc):
        self.nc = nc
        self.ops = {e: [] for e in ENGS}
        self.dmaops = {e: [] for e in ENGS}
        self.last_w = {}
        self.readers = {}
        self.pending = {e: [] for e in ENGS}

    def op(self, eng, fn, reads=(), writes=(), dma=False):
        rd, wr = [], list(writes)
        for r in reads:
            nm = r[0] if isinstance(r, tuple) else r
            if nm in PSUM_NAMES:
                wr.append(r)
            else:
                rd.append(r)
        reads, writes = rd, wr
        o = _Op(eng, fn, dma)
        o.idx = len(self.ops[eng])
        deps = list(self.pending[eng])
        self.pending[eng] = []
        for r in reads:
            w = self.last_w.get(r)
            if w is not None:
                deps.append(w)
        for wkey in writes:
            w = self.last_w.get(wkey)
            if w is not None:
                deps.append(w)
            deps.extend(self.readers.get(wkey, ()))
        seen = set()
        for d in deps:
            if d is o or id(d) in seen:
                continue
            seen.add(id(d))
            if not d.is_dma and d.eng == eng:
                if eng in ("pe", "sp"):
                    continue
            o.deps.append(d)
            d.has_dep = True
        if dma:
            n = len(self.dmaops[eng])
            o.dma_slot = n % NDMASEM
            o.dma_val = 16 * (n // NDMASEM + 1)
            if n >= NDMASEM:
                o.deps.append(self.dmaops[eng][n - NDMASEM])
            self.dmaops[eng].append(o)
        self.ops[eng].append(o)
        for r in reads:
            self.readers.setdefault(r, []).append(o)
        for wkey in writes:
            self.last_w[wkey] = o
            self.readers[wkey] = []
        return o

    def barrier(self):
        lasts = []
        for e in ENGS:
            nd = [x for x in self.ops[e] if not x.is_dma]
            if nd:
                lasts.append(nd[-1])
            lasts.extend(self.dmaops[e][-NDMASEM:])
        for e in ENGS:
            self.pending[e] = list(lasts)

    def emit(self, sems, dmasems):
        nc = self.nc
        for e in ENGS:
            c = 0
            for o in self.ops[e]:
                if o.is_dma:
                    continue
                if o.has_dep:
                    c += 1
                    o.sig = c

        def mk(e):
            def body(eng):
                waited = {}
                for o in self.ops[e]:
                    for d in o.deps:
                        if d.is_dma:
                            key = ("d", d.eng, d.dma_slot)
                            val = d.dma_val
                            sem = dmasems[d.eng][d.dma_slot]
                        else:
                            key = ("c", d.eng)
                            val = d.sig
                            sem = sems[d.eng]
                        if waited.get(key, 0) >= val:
                            continue
                        waited[key] = val
                        eng.wait_ge(sem, val)
                    ins = o.fn(eng)
                    if o.is_dma:
                        ins.then_inc(dmasems[e][o.dma_slot], 16)
                    elif o.has_dep:
                        ins.then_inc(sems[e], 1)
                lastvals = {}
                for x in self.dmaops[e]:
                    lastvals[x.dma_slot] = x.dma_val
                for s, v in lastvals.items():
                    eng.wait_ge(dmasems[e][s], v)
            return body

        with nc.Block() as block:
            block.tensor(mk("pe"))
            block.scalar(mk("act"))
            block.vector(mk("dve"))
            block.gpsimd(mk("pool"))
            block.sync(mk("sp"))


def build_nc(phases="ABCD", debug=False):
    nc = bass.Bass("TRN2", target_bir_lowering=False)
    dkind = "ExternalOutput" if debug else None

    def dram(name, shape, dt, kind=None):
        if kind is None:
            return nc.dram_tensor(name, shape, dt).ap()
        return nc.dram_tensor(name, shape, dt, kind=kind).ap()

    x_d = dram("x", [L, D], F32, "ExternalInput")
    w_in = dram("w_in", [D, NCOLS], F32, "ExternalInput")
    w_bs = dram("w_bs", [D, D], F32, "ExternalInput")
    w_ba = dram("w_ba", [D, D], F32, "ExternalInput")
    w_o = dram("w_o", [D, D], F32, "ExternalInput")
    pre_g = dram("pre_g", [128, 8], F32, "ExternalInput")
    post_g = dram("post_g", [D], F32, "ExternalInput")
    norm_g = dram("norm_g", [128, 8], F32, "ExternalInput")
    conv_w = dram("conv_w", [128, 16, 4], F32, "ExternalInput")
    conv_b = dram("conv_b", [128, 16], F32, "ExternalInput")
    dt_bias = dram("dt_bias", [16], F32, "ExternalInput")
    a_log = dram("a_log", [16], F32, "ExternalInput")
    d_skip = dram("d_skip", [16], F32, "ExternalInput")
    fgate_b = dram("fgate_b", [16], F32, "ExternalInput")
    consts = dram("consts", [128, 4, 128], F32, "ExternalInput")
    sel_d = dram("sel", [16, 16, 128], F32, "ExternalInput")
    out_d = dram("out", [L, D], F32, "ExternalOutput")
    ya_sp = dram("ya_sp", [8, 128, L], BF16, dkind)
    ys_sp = dram("ys_sp", [8, 128, L], BF16, dkind)

    dbg_g = dram("dbg_g", [128, 4, 512], F32, "ExternalOutput") if debug else None
    S = Sched(nc)
    with ExitStack() as es:
        def sb(name, shape, dt, ctx=None):
            return (ctx or es).enter_context(nc.sbuf_tensor(name, shape, dt))

        def ps(name, shape, dt, ctx=None):
            return (ctx or es).enter_context(nc.psum_tensor(name, shape, dt))

        sems = {e: es.enter_context(nc.semaphore("s_" + e)) for e in ENGS}
        dmasems = {e: [es.enter_context(nc.semaphore(f"d_{e}{i}")) for i in range(NDMASEM)] for e in ENGS}

        def dma(out, in_, reads=(), writes=(), q="sp", **kw):
            return S.op(q, lambda e: e.dma_start(out=out, in_=in_, **kw), reads=reads, writes=writes, dma=True)

        hT = sb("hT", [128, 8, L], BF16)
        cst = sb("cst", [128, 4, 128], F32)
        ident_b = sb("ident_b", [128, 128], BF16)
        tri_b = sb("tri_b", [128, 128], BF16)
        gT = sb("gT", [128, 8], F32)
        ngT = sb("ngT", [128, 8], F32)
        eps_t = sb("eps_t", [128, 1], F32)
        ident_f = cst[:, 0, :]
        tri_f = cst[:, 1, :]
        sgt_f = cst[:, 2, :]
        ones_f = cst[:, 3, :]

        dma(cst[:], consts[:, :, :], writes=["cst"])
        dma(gT[:], pre_g[:, :], writes=["gT"])
        dma(ngT[:], norm_g[:, :], writes=["ngT"])
        S.op("pool", lambda e: e.memset(eps_t[:], EPS), writes=["eps"])
        S.op("pool", lambda e: e.tensor_copy(out=ident_b[:], in_=ident_f), reads=["cst"], writes=["ident_b"])
        S.op("pool", lambda e: e.tensor_copy(out=tri_b[:], in_=tri_f), reads=["cst"], writes=["tri_b"])

        wcount = [0]
        def load_weight(dst, dst_key, src_slices, stg, stg_key, gain, gain_key, eng="pool", defer=False):
            if isinstance(stg, list):
                j = wcount[0] % len(stg)
                wcount[0] += 1
                stg, stg_key = stg[j], (stg_key, j)
            n_tot = 0
            for (src, off) in src_slices:
                n = src.shape[1]
                dma(stg[:, :, off:off + n], src.rearrange("(k p) n -> p k n", p=128), writes=[stg_key])
                n_tot = max(n_tot, off + n)
            casts = []
            for k in range(8):
                def cast(k=k):
                    if gain is not None:
                        S.op(eng, lambda e: e.tensor_scalar(out=dst[:, k, 0:n_tot], in0=stg[:, k, 0:n_tot], scalar1=gain[:, k:k + 1],
                                                            scalar2=None, op0=ALU.mult),
                             reads=[stg_key, gain_key], writes=[dst_key])
                    else:
                        S.op(eng, lambda e: e.tensor_copy(out=dst[:, k, 0:n_tot], in_=stg[:, k, 0:n_tot]),
                             reads=[stg_key], writes=[dst_key])
                casts.append(cast)
            if defer:
                return casts
            for c in casts:
                c()
            return []

        with ExitStack() as pa:
            xt = [sb(f"xt{i}", [128, D], F32, pa) for i in range(2)]
            sq = sb("sq", [128, D], F32, pa)
            ss = [sb(f"ss{i}", [128, 1], F32, pa) for i in range(2)]
            rs = [sb(f"rs{i}", [128, 1], F32, pa) for i in range(2)]
            hb = [sb(f"hb{i}", [128, D], BF16, pa) for i in range(2)]
            pt = [ps(f"pt{i}", [128, 8, 128], BF16, pa) for i in range(2)]
            for tb in range(NTB):
                i = tb % 2
                dma(xt[i][:], x_d[tb * 128:(tb + 1) * 128, :], writes=[("xt", i)])
                S.op("act", lambda e, i=i: e.activation(out=sq[:], in_=xt[i][:], func=AF.Square, accum_out=ss[i][:]),
                     reads=[("xt", i)], writes=["sq", ("ss", i)])
                S.op("act", lambda e, i=i: e.activation(out=rs[i][:], in_=ss[i][:], func=AF.Ln, scale=1.0 / D, bias=eps_t[:, 0:1]),
                     reads=[("ss", i), "eps"], writes=[("rs", i)])
                S.op("act", lambda e, i=i: e.activation(out=rs[i][:], in_=rs[i][:], func=AF.Exp, scale=-0.5),
                     reads=[("rs", i)], writes=[("rs", i)])
                S.op("dve", lambda e, i=i: e.tensor_scalar(out=hb[i][:], in0=xt[i][:], scalar1=rs[i][:, 0:1], scalar2=None, op0=ALU.mult),
                     reads=[("rs", i), ("xt", i)], writes=[("hb", i)])
                for k in range(8):
                    S.op("pe", lambda e, i=i, k=k: e.transpose(out=pt[i][:, k, :], in_=hb[i][:, k * 128:(k + 1) * 128], identity=ident_b[:]),
                         reads=[("hb", i), "ident_b"], writes=[("pt", i)])
                S.op("pool" if False else "dve", lambda e, i=i, tb=tb: e.tensor_copy(out=hT[:, :, tb * 128:(tb + 1) * 128], in_=pt[i][:]),
                     reads=[("pt", i)], writes=[("hT", tb)])
        S.barrier()

        def hT_keys(t0, t1):
            return [("hT", tb) for tb in range(t0 // 128, (t1 + 127) // 128)]

        if "B" in phases:
            with ExitStack() as pb:
                wst = sb("b_wst", [128, 8, 512], F32, pb)
                wb = sb("b_wb", [128, 8, 512], BF16, pb)
                wfs = sb("b_wfs", [128, 8, 16], F32, pb)
                wfb = sb("b_wfb", [128, 8, 16], BF16, pb)
                qT = sb("b_qT", [128, L], BF16, pb)
                kT = sb("b_kT", [128, L], BF16, pb)
                v1 = sb("b_v1", [128, NTB, 2, 65], BF16, pb)
                zs = sb("b_zs", [128, NTB, 128], BF16, pb)
                yat = sb("b_yat", [128, NTB, 128], BF16, pb)
                yTs = sb("b_yTs", [128, L], BF16, pb)
                T1 = [sb(f"b_T1{i}", [128, 512], F32, pb) for i in range(6)]
                PT = [sb(f"b_PT{i}", [128, 512], BF16, pb) for i in range(8)]
                Ftb = [sb(f"b_Ftb{i}", [128, 512], F32, pb) for i in range(4)]
                gend = [sb(f"b_gend{i}", [128, 1], F32, pb) for i in range(4)]
                kbias = [sb(f"b_kbias{i}", [128, NTB], F32, pb) for i in range(4)]
                rden = [sb(f"b_rden{i}", [128, 4, 1], F32, pb) for i in range(2)]
                otmp = [sb(f"b_otmp{i}", [128, 4, 64], F32, pb) for i in range(2)]
                fgb = sb("b_fgb", [128, 16], F32, pb)
                zt = sb("b_zt", [128, 260], BF16, pb)
                spf = sb("b_spf", [128, NTB, 16], F32, pb)
                tot = sb("b_tot", [128, NTB, 16], F32, pb)
                pre = sb("b_pre", [128, NTB, 16], F32, pb)
                G_TM = sb("b_GTM", [128, NTB, 16], F32, pb)
                G_FM = sb("b_GFM", [16, L], F32, pb)
                sel = sb("b_sel", [16, 16, 128], F32, pb)
                pp = [ps(f"b_pp{i}", [128, 512], F32, pb) for i in range(NPP)]
                pst = pp
                pO = [ps(f"b_pO{i}", [128, 512], F32, pb) for i in range(2)]
                pg = pp[2]

                dma(fgb[:], fgate_b.partition_broadcast(128), writes=["fgb"])
                dma(sel[:], sel_d[:, :, :], writes=["sel"])
                S.op("pool", lambda e: e.memset(v1[:, :, :, 64:65], 1.0), writes=["v1ones"])
                S.op("pool", lambda e: e.memset(zt[:], 0.0), writes=["zt"])

                load_weight(wfb, "wfb", [(w_in[:, O_F:O_F + 16], 0)], wfs, "wfs", gT, "gT")
                for tb in range(NTB):
                    for k in range(8):
                        S.op("pe", lambda e, tb=tb, k=k: e.matmul(pg[:, tb * 16:(tb + 1) * 16], lhsT=hT[:, k, tb * 128:(tb + 1) * 128],
                                                               rhs=wfb[:, k, :], start=(k == 0), stop=(k == 7)),
                             reads=[("hT", tb), "wfb"], writes=[("pp", 2)])
                S.op("dve", lambda e: e.tensor_tensor(out=spf[:], in0=pg[:].rearrange("p (a b) -> p a b", b=16),
                                                      in1=fgb[:].unsqueeze(1).to_broadcast([128, NTB, 16]), op=ALU.add),
                     reads=[("pp", 2), "fgb"], writes=["spf"])
                S.op("act", lambda e: e.activation(out=spf[:], in_=spf[:], func=AF.Exp, scale=-1.0), reads=["spf"], writes=["spf"])
                S.op("act", lambda e: e.activation(out=spf[:], in_=spf[:], func=AF.Ln, bias=1.0), reads=["spf"], writes=["spf"])
                npp = [0]
                nst = [0]
                nT1 = [0]
                nPT = [0]
                nO = [0]
                nF = [0]
                def load_pair_weights(hp, defer=False):
                    return load_weight(wb, "b_wb", [(w_in[:, O_Q + hp * 128:O_Q + (hp + 1) * 128], 0),
                                                    (w_in[:, O_K + hp * 128:O_K + (hp + 1) * 128], 128),
                                                    (w_in[:, O_V + hp * 128:O_V + (hp + 1) * 128], 256),
                                                    (w_in[:, O_ZA + hp * 128:O_ZA + (hp + 1) * 128], 384)], wst, "b_wst", gT, "gT", eng="dve", defer=defer)

                def proj_qk(hp):
                    for (dst, dkey, c0) in ((qT, "qT", 0), (kT, "kT", 128)):
                        for sl in range(NSLAB):
                            i = npp[0] % NPP
                            npp[0] += 1
                            for k in range(8):
                                S.op("pe", lambda e, i=i, k=k, sl=sl, c0=c0: e.matmul(pp[i][:], lhsT=wb[:, k, c0:c0 + 128], rhs=hT[:, k, sl * 512:(sl + 1) * 512],
                                                                                    start=(k == 0), stop=(k == 7)),
                                     reads=hT_keys(sl * 512, (sl + 1) * 512) + ["b_wb"], writes=[("pp", i)])
                            S.op("act" if sl % 2 == 0 else "dve",
                                 (lambda e, i=i, sl=sl, dst=dst: e.copy(out=dst[:, sl * 512:(sl + 1) * 512], in_=pp[i][:])) if sl % 2 == 0 else
                                 (lambda e, i=i, sl=sl, dst=dst: e.tensor_copy(out=dst[:, sl * 512:(sl + 1) * 512], in_=pp[i][:])),
                                 reads=[("pp", i)], writes=[(dkey, sl)])

                def proj_vz(hp):
                    for tb in range(NTB):
                        i = npp[0] % NPP
                        npp[0] += 1
                        for k in range(8):
                            S.op("pe", lambda e, i=i, k=k, tb=tb: e.matmul(pp[i][:, 0:256], lhsT=hT[:, k, tb * 128:(tb + 1) * 128], rhs=wb[:, k, 256:512],
                                                                         start=(k == 0), stop=(k == 7)),
                                 reads=[("hT", tb), "b_wb"], writes=[("pp", i)])
                        S.op("dve", lambda e, i=i, tb=tb: e.tensor_copy(out=v1[:, tb, :, 0:64], in_=pp[i][:, 0:128].rearrange("p (a b) -> p a b", b=64)),
                             reads=[("pp", i)], writes=[("v1", tb)])
                        S.op("act", lambda e, i=i, tb=tb: e.activation(out=zs[:, tb, :], in_=pp[i][:, 128:256], func=AF.Silu),
                             reads=[("pp", i)], writes=[("zs", tb)])

                def spill_pair(hp):
                    for sp_ in range(NSLAB):
                        i = npp[0] % NPP
                        npp[0] += 1
                        ptb = pp[i][:].bitcast(BF16)
                        for j in range(4):
                            tb = 4 * sp_ + j
                            S.op("pe", lambda e, ptb=ptb, j=j, tb=tb: e.transpose(out=ptb[:, j * 128:(j + 1) * 128], in_=yat[:, tb, :], identity=ident_b[:]),
                                 reads=[("yat", sp_), "ident_b"], writes=[("pp", i)])
                        S.op("dve", lambda e, ptb=ptb, sp_=sp_: e.tensor_copy(out=yTs[:, sp_ * 512:(sp_ + 1) * 512], in_=ptb[:, 0:512]),
                             reads=[("pp", i)], writes=[("yTs", sp_)])
                    dma(ya_sp[hp, :, :], yTs[:], reads=[("yTs", s_) for s_ in range(NSLAB)], writes=[("ya_sp", hp)])

                def forget_rest():
                    spf2 = spf[:].rearrange("p a b -> p (a b)")
                    S.op("pe", lambda e: e.matmul(pp[0][:], lhsT=tri_f, rhs=spf2, start=True, stop=True), reads=["spf", "cst"], writes=[("pp", 0)])
                    S.op("pe", lambda e: e.matmul(pp[1][:], lhsT=ones_f, rhs=spf2, start=True, stop=True), reads=["spf", "cst"], writes=[("pp", 1)])
                    S.op("dve", lambda e: e.tensor_copy(out=tot[:].rearrange("p a b -> p (a b)"), in_=pp[1][:]), reads=[("pp", 1)], writes=["tot"])
                    S.op("dve", lambda e: e.memset(pre[:, 0, :], 0.0), writes=["pre"])
                    for b in range(1, NTB):
                        S.op("dve", lambda e, b=b: e.tensor_tensor(out=pre[:, b, :], in0=pre[:, b - 1, :], in1=tot[:, b - 1, :], op=ALU.add),
                             reads=["pre", "tot"], writes=["pre"])
                    S.op("dve", lambda e: e.tensor_tensor(out=G_TM[:].rearrange("p a b -> p (a b)"), in0=pp[0][:],
                                                          in1=pre[:].rearrange("p a b -> p (a b)"), op=ALU.add),
                         reads=[("pp", 0), "pre"], writes=["G_TM"])
                    if debug:
                        dma(dbg_g[:, 0, :], spf[:].rearrange("p a b -> p (a b)"), reads=["spf"])
                        dma(dbg_g[:, 1, :], tot[:].rearrange("p a b -> p (a b)"), reads=["tot"])
                        dma(dbg_g[:, 2, :], pre[:].rearrange("p a b -> p (a b)"), reads=["pre"])
                        dma(dbg_g[:, 3, :], G_TM[:].rearrange("p a b -> p (a b)"), reads=["G_TM"])
                    pgT = pp[0][0:16, :]
                    for g4 in range(8):
                        for j in range(4):
                            tb = g4 * 4 + j
                            S.op("pe", lambda e, tb=tb, j=j: e.transpose(out=pgT[:, j * 128:(j + 1) * 128], in_=G_TM[:, tb, :], identity=ident_f),
                                 reads=["G_TM", "cst"], writes=[("pp", 0)])
                        S.op("dve", lambda e, g4=g4: e.tensor_copy(out=G_FM[:, g4 * 512:(g4 + 1) * 512], in_=pgT), reads=[("pp", 0)], writes=["G_FM"])


                load_pair_weights(0)
                proj_qk(0)
                proj_vz(0)
                forget_rest()
                for hp in range(8):
                    pending_casts = load_pair_weights(hp + 1, defer=True) if hp < 7 else []
                    LA = 2
                    jobs = []
                    for sp_ in range(NSLAB):
                        for kb in range(4 * sp_ + 4):
                            jobs.append((sp_, kb))
                    st = {}

                    def stage1(job):
                        sp_, kb = job
                        nkb = 4 * sp_ + 4
                        if kb == 0:
                            for hl in range(2):
                                h = hp * 2 + hl
                                fi = (sp_ % 2) * 2 + hl
                                gi = nst[0] % NPP
                                nst[0] += 1
                                S.op("pe", lambda e, h=h, gi=gi: e.matmul(pp[gi][:], lhsT=sel[:, h, :], rhs=G_FM[:, sp_ * 512:(sp_ + 1) * 512], start=True, stop=True),
                                     reads=["sel", "G_FM"], writes=[("pp", gi)])
                                S.op("act", lambda e, fi=fi, gi=gi: e.copy(out=gend[fi][:], in_=pp[gi][:, 511:512]), reads=[("pp", gi)], writes=[("gend", fi)])
                                S.op("dve", lambda e, fi=fi, gi=gi: e.tensor_scalar(out=Ftb[fi][:], in0=pp[gi][:], scalar1=gend[fi][:, 0:1], scalar2=-1.0,
                                                                                    op0=ALU.subtract, op1=ALU.mult),
                                     reads=[("pp", gi), ("gend", fi)], writes=[("Ftb", fi)])
                                S.op("dve", lambda e, fi=fi, h=h: e.tensor_scalar(out=kbias[fi][:, 0:nkb], in0=G_TM[:, 0:nkb, h], scalar1=gend[fi][:, 0:1],
                                                                                  scalar2=None, op0=ALU.subtract),
                                     reads=["G_TM", ("gend", fi)], writes=[("kbias", fi)])
                        m = kb - 4 * sp_
                        c0 = max(m, 0) * 128
                        info = []
                        for hl in range(2):
                            si = nst[0] % NPP
                            nst[0] += 1
                            ti = nT1[0] % 6
                            nT1[0] += 1
                            pi = nPT[0] % 8
                            nPT[0] += 1
                            info.append((si, ti, pi))
                        st[job] = info
                        for hl in range(2):
                            p0 = hl * 64
                            si = info[hl][0]
                            S.op("pe", lambda e, p0=p0, si=si: e.matmul(
                                pst[si][:, c0:512], lhsT=kT[p0:p0 + 64, kb * 128:(kb + 1) * 128],
                                rhs=qT[p0:p0 + 64, sp_ * 512 + c0:(sp_ + 1) * 512], start=True, stop=True),
                                reads=[("kT", kb // 4), ("qT", sp_)], writes=[("pp", si)])
                        for hl in range(2):
                            fi = (sp_ % 2) * 2 + hl
                            si, ti, pi = info[hl]
                            S.op("dve", lambda e, si=si, ti=ti, fi=fi: e.scalar_tensor_tensor(
                                out=T1[ti][:, c0:512], in0=pst[si][:, c0:512], scalar=0.125, in1=Ftb[fi][:, c0:512], op0=ALU.mult, op1=ALU.add),
                                reads=[("pp", si), ("Ftb", fi)], writes=[("T1", ti)])
                            S.op("act", lambda e, ti=ti, pi=pi, fi=fi: e.activation(
                                out=PT[pi][:, c0:512], in_=T1[ti][:, c0:512], func=AF.Exp, bias=kbias[fi][:, kb:kb + 1]),
                                reads=[("T1", ti), ("kbias", fi)], writes=[("PT", pi)])
                            if m >= 0:
                                S.op("pool", lambda e, pi=pi: e.tensor_tensor(out=PT[pi][:, c0:c0 + 128], in0=PT[pi][:, c0:c0 + 128],
                                                                              in1=tri_b[:], op=ALU.mult),
                                     reads=[("PT", pi), "tri_b"], writes=[("PT", pi)])

                    def stage2(job):
                        sp_, kb = job
                        m = kb - 4 * sp_
                        for hl in range(2):
                            p0 = hl * 64
                            oi = hl
                            pi = st[job][hl][2]
                            if kb == 0:
                                S.op("pe", lambda e, oi=oi: e.matmul(pO[oi][:, 0:260], lhsT=zt[:, 0:128], rhs=zt[:, 0:260],
                                                                     start=True, stop=False),
                                     reads=["zt"], writes=[("pO", oi)])
                            for ql in range(max(m, 0), 4):
                                S.op("pe", lambda e, ql=ql, oi=oi, pi=pi, hl=hl: e.matmul(
                                    pO[oi][:, ql * 65:(ql + 1) * 65], lhsT=PT[pi][:, ql * 128:(ql + 1) * 128], rhs=v1[:, kb, hl, :],
                                    start=False, stop=(kb == 4 * sp_ + 3 and ql == 3)),
                                    reads=[("PT", pi), ("v1", kb), "v1ones"], writes=[("pO", oi)])
                            if kb == 4 * sp_ + 3:
                                S.op("dve", lambda e, oi=oi: e.reciprocal(out=rden[oi][:], in_=pO[oi][:, 0:260].rearrange("p (a b) -> p a b", b=65)[:, :, 64:65]), reads=[("pO", oi)], writes=[("rden", oi)])
                                S.op("dve", lambda e, oi=oi: e.tensor_tensor(out=otmp[oi][:], in0=pO[oi][:, 0:260].rearrange("p (a b) -> p a b", b=65)[:, :, 0:64], in1=rden[oi][:].to_broadcast([128, 4, 64]),
                                                                             op=ALU.mult),
                                     reads=[("pO", oi), ("rden", oi)], writes=[("otmp", oi)])
                                S.op("pool", lambda e, oi=oi, p0=p0: e.tensor_tensor(
                                    out=yat[:, 4 * sp_:4 * sp_ + 4, p0:p0 + 64], in0=otmp[oi][:], in1=zs[:, 4 * sp_:4 * sp_ + 4, p0:p0 + 64], op=ALU.mult),
                                    reads=[("otmp", oi)] + [("zs", tb) for tb in range(4 * sp_, 4 * sp_ + 4)], writes=[("yat", sp_)])

                    for idx in range(len(jobs) + LA):
                        if idx < len(jobs):
                            stage1(jobs[idx])
                        if idx >= LA:
                            stage2(jobs[idx - LA])
                        if pending_casts and idx >= 16:
                            pending_casts.pop(0)()
                    while pending_casts:
                        pending_casts.pop(0)()
                    if hp < 7:
                        proj_qk(hp + 1)
                    spill_pair(hp)
                    if hp < 7:
                        proj_vz(hp + 1)
            S.barrier()

        if "C" in phases:
            with ExitStack() as pc:
                wst = [sb(f"c_wst{i}", [128, 8, 128], F32, pc) for i in range(2)]
                Wz = sb("c_Wz", [128, 8, 1024], BF16, pc)
                Wx = sb("c_Wx", [128, 8, 2048], BF16, pc)
                Wdt = sb("c_Wdt", [128, 8, 16], BF16, pc)
                cw = sb("c_cw", [128, 16, 4], F32, pc)
                cb = sb("c_cb", [128, 16], F32, pc)
                dtb = sb("c_dtb", [128, 16], F32, pc)
                a_b = sb("c_ab", [128, 16], F32, pc)
                dsk = sb("c_dsk", [128, 16], F32, pc)
                halo = sb("c_halo", [128, 16, 3], F32, pc)
                ucur = [sb(f"c_u{i}", [128, 515], F32, pc) for i in range(2)]
                xc = sb("c_xc", [128, 16, 512], BF16, pc)
                zsl = sb("c_zs", [128, 4, D], BF16, pc)
                xsT = [sb(f"c_xsT{i}", [128, D], BF16, pc) for i in range(2)]
                xdt = [sb(f"c_xdt{i}", [128, D], BF16, pc) for i in range(2)]
                xdte = [sb(f"c_xdte{i}", [128, D], BF16, pc) for i in range(2)]
                B_TM = [sb(f"c_BTM{i}", [128, 4, 128], BF16, pc) for i in range(2)]
                MT = [sb(f"c_MT{i}", [128, 16, 128], BF16, pc) for i in range(2)]
                ecs = [sb(f"c_ecs{i}", [128, 16], F32, pc) for i in range(2)]
                cdb = [sb(f"c_cdb{i}", [128, 16], F32, pc) for i in range(2)]
                dt_t = sb("c_dt", [128, 16], F32, pc)
                dA = sb("c_dA", [128, 16], F32, pc)
                R1 = sb("c_R1", [128, 8, 128], F32, pc)
                dec = sb("c_dec", [128, 16, 128], BF16, pc)
                cbm = sb("c_cbm", [128, 4, 128], BF16, pc)
                y1s = [sb(f"c_y1{i}", [128, D], F32, pc) for i in range(2)]
                y2 = sb("c_y2", [128, D], F32, pc)
                acc = [y2[:, 0:512], y2[:, 512:1024]]
                prev = sb("c_prev", [128, D], F32, pc)
                prevb = sb("c_prevb", [128, D], BF16, pc)
                gss = sb("c_gss", [128, 4], F32, pc)
                grs = sb("c_grs", [128, 4], F32, pc)
                ysb = sb("c_ysb", [128, D], BF16, pc)
                ysT = [sb(f"c_ysT{i}", [128, 8, 128], BF16, pc) for i in range(2)]
                ppx = [ps(f"c_ppx{i}", [128, 512], F32, pc) for i in range(2)]
                pseg = [ps(f"c_pseg{i}", [128, 512], F32, pc) for i in range(2)]
                pcb = ps("c_pcb", [128, 512], F32, pc)
                pyd = ps("c_pyd", [128, 512], F32, pc)
                pyo = ps("c_pyo", [128, 512], F32, pc)
                ptr = ps("c_ptr", [128, 1024], BF16, pc)

                dma(cw[:], conv_w[:, :, :], writes=["cw"])
                dma(cb[:], conv_b[:, :], writes=["cb"])
                dma(dtb[:], dt_bias.partition_broadcast(128), writes=["dtb"])
                dma(a_b[:], a_log.partition_broadcast(128), writes=["a_b"])
                dma(dsk[:], d_skip.partition_broadcast(128), writes=["dsk"])
                S.op("act", lambda e: e.activation(out=a_b[:], in_=a_b[:], func=AF.Exp), reads=["a_b"], writes=["a_b"])
                S.op("dve", lambda e: e.tensor_scalar(out=a_b[:], in0=a_b[:], scalar1=-1.0, scalar2=None, op0=ALU.mult), reads=["a_b"], writes=["a_b"])
                S.op("pool", lambda e: e.memset(halo[:], 0.0), writes=["halo"])
                S.op("pool", lambda e: e.memset(prev[:], 0.0), writes=[("prev", 0), ("prev", 1)])
                S.op("pool", lambda e: e.memset(prevb[:], 0.0), writes=[("prevb", 0), ("prevb", 1)])
                def load_wx(c4):
                    load_weight(Wx[:, :, c4 * 128:(c4 + 1) * 128], ("Wx", c4), [(w_in[:, O_XBC + c4 * 128:O_XBC + (c4 + 1) * 128], 0)], wst, "c_wst", gT, "gT", eng="dve")

                def load_wz(c4):
                    load_weight(Wz[:, :, c4 * 128:(c4 + 1) * 128], ("Wz", c4 // 4), [(w_in[:, O_ZS + c4 * 128:O_ZS + (c4 + 1) * 128], 0)], wst, "c_wst", gT, "gT", eng="dve")

                load_wx(0)
                load_wx(1)
                nu = [0]

                def slab_front(sl):
                    t0 = sl * 512
                    hk = hT_keys(t0, t0 + 512)
                    bufi = {}

                    def evac_stage(cc):
                        i = nu[0] % 2
                        nu[0] += 1
                        bufi[cc] = i
                        if sl == 0:
                            if cc + 2 < 16:
                                load_wx(cc + 2)
                            elif cc == 14:
                                load_weight(Wdt, "Wdt", [(w_in[:, O_DT:O_DT + 16], 0)], wst, "c_wst", gT, "gT", eng="dve")
                            if cc >= 8:
                                load_wz(cc - 8)
                        for k in range(8):
                            S.op("pe", lambda e, i=i, k=k, cc=cc: e.matmul(ppx[i][:], lhsT=Wx[:, k, cc * 128:(cc + 1) * 128], rhs=hT[:, k, t0:t0 + 512],
                                                                         start=(k == 0), stop=(k == 7)),
                                 reads=hk + [("Wx", cc)], writes=[("ppx", i)])
                        S.op("act", lambda e, i=i: e.copy(out=ucur[i][:, 3:515], in_=ppx[i][:]), reads=[("ppx", i)], writes=[("u", i)])
                        S.op("act", lambda e, i=i, cc=cc: e.copy(out=ucur[i][:, 0:3], in_=halo[:, cc, :]), reads=["halo"], writes=[("u", i)])
                        S.op("act", lambda e, i=i, cc=cc: e.copy(out=halo[:, cc, :], in_=ucur[i][:, 512:515]), reads=[("u", i)], writes=["halo"])

                    def conv_stage(cc):
                        i = bufi[cc]
                        S.op("dve", lambda e, i=i, cc=cc: e.tensor_scalar(out=acc[i], in0=ucur[i][:, 0:512], scalar1=cw[:, cc, 0:1], scalar2=None, op0=ALU.mult),
                             reads=[("u", i), "cw"], writes=[("y2", i)])
                        for tap in range(1, 4):
                            S.op("dve", lambda e, i=i, cc=cc, tap=tap: e.scalar_tensor_tensor(out=acc[i], in0=ucur[i][:, tap:tap + 512], scalar=cw[:, cc, tap:tap + 1],
                                                                                              in1=acc[i], op0=ALU.mult, op1=ALU.add),
                                 reads=[("u", i), "cw", ("y2", i)], writes=[("y2", i)])
                        S.op("act", lambda e, i=i, cc=cc: e.activation(out=xc[:, cc, :], in_=acc[i], func=AF.Silu, bias=cb[:, cc:cc + 1]),
                             reads=[("y2", i), "cb"], writes=[("xc", cc)])

                    evac_stage(0)
                    for cc in range(16):
                        if cc + 1 < 16:
                            evac_stage(cc + 1)
                        conv_stage(cc)
                    for ch in range(4):
                        tb = sl * 4 + ch
                        for hf in range(2):
                            i = nu[0] % 2
                            nu[0] += 1
                            for k in range(8):
                                S.op("pe", lambda e, i=i, k=k, tb=tb, hf=hf: e.matmul(ppx[i][:], lhsT=hT[:, k, tb * 128:(tb + 1) * 128], rhs=Wz[:, k, hf * 512:(hf + 1) * 512],
                                                                                     start=(k == 0), stop=(k == 7)),
                                     reads=[("hT", tb), ("Wz", hf)], writes=[("ppx", i)])
                            S.op("act", lambda e, i=i, hf=hf, ch=ch: e.activation(out=zsl[:, ch, hf * 512:(hf + 1) * 512], in_=ppx[i][:], func=AF.Silu),
                                 reads=[("ppx", i)], writes=[("zsl", ch)])

                def front(sl, ch):
                    c0 = ch * 128
                    tb = sl * 4 + ch
                    b = tb % 2
                    i = nu[0] % 2
                    nu[0] += 1
                    for k in range(8):
                        S.op("pe", lambda e, k=k: e.matmul(ppx[i][:, 0:16], lhsT=hT[:, k, tb * 128:(tb + 1) * 128], rhs=Wdt[:, k, :], start=(k == 0), stop=(k == 7)),
                             reads=[("hT", tb), "Wdt"], writes=[("ppx", i)])
                    S.op("dve", lambda e: e.tensor_tensor(out=dt_t[:], in0=ppx[i][:, 0:16], in1=dtb[:], op=ALU.add), reads=[("ppx", i), "dtb"], writes=["dt"])
                    S.op("act", lambda e: e.activation(out=dt_t[:], in_=dt_t[:], func=AF.Exp), reads=["dt"], writes=["dt"])
                    S.op("act", lambda e: e.activation(out=dt_t[:], in_=dt_t[:], func=AF.Ln, bias=1.0), reads=["dt"], writes=["dt"])
                    S.op("dve", lambda e: e.tensor_tensor(out=dA[:], in0=dt_t[:], in1=a_b[:], op=ALU.mult), reads=["dt", "a_b"], writes=["dA"])
                    yield
                    for k in range(8):
                        S.op("pe", lambda e, k=k: e.transpose(out=ptr[:, k * 128:(k + 1) * 128], in_=xc[:, k, c0:c0 + 128], identity=ident_b[:]),
                             reads=[("xc", k), "ident_b"], writes=["ptr"])
                    S.op("act", lambda e: e.copy(out=xsT[b][:], in_=ptr[:]), reads=["ptr"], writes=[("xsT", b)])
                    S.op("dve", lambda e: e.tensor_tensor(out=xdt[b][:].rearrange("p (h d) -> p h d", d=64), in0=ptr[:].rearrange("p (h d) -> p h d", d=64),
                                                          in1=dt_t[:].unsqueeze(2).to_broadcast([128, 16, 64]), op=ALU.mult),
                         reads=["ptr", "dt"], writes=[("xdt", b)])
                    yield
                    for g in range(4):
                        S.op("pe", lambda e, g=g: e.transpose(out=ptr[:, g * 128:(g + 1) * 128], in_=xc[:, 8 + g, c0:c0 + 128], identity=ident_b[:]),
                             reads=[("xc", 8 + g), "ident_b"], writes=["ptr"])
                    S.op("act", lambda e: e.copy(out=B_TM[b][:].rearrange("p g n -> p (g n)"), in_=ptr[:, 0:512]), reads=["ptr"], writes=[("B_TM", b)])
                    yield
                    for r2 in range(2):
                        S.op("dve", lambda e, r2=r2: e.tensor_tensor(out=R1[:], in0=dA[:, 8 * r2:8 * r2 + 8].unsqueeze(2).to_broadcast([128, 8, 128]),
                                                                      in1=tri_f.unsqueeze(1).to_broadcast([128, 8, 128]), op=ALU.mult),
                             reads=["dA", "cst"], writes=["R1"])
                        for q2 in range(2):
                            q4 = 2 * r2 + q2
                            j = q4 % 2
                            S.op("pe", lambda e, j=j, q2=q2: e.matmul(pseg[j][:], lhsT=sgt_f, rhs=R1[:, 4 * q2:4 * q2 + 4, :], start=True, stop=True),
                                 reads=["R1", "cst"], writes=[("pseg", j)])
                            S.op("act", lambda e, j=j, q4=q4: e.activation(out=dec[:, 4 * q4:4 * q4 + 4, :], in_=pseg[j][:].rearrange("p (h l) -> p h l", l=128), func=AF.Exp),
                                 reads=[("pseg", j)], writes=[("dec", q4)])
                    yield
                    S.op("pe", lambda e: e.matmul(pcb[:, 0:16], lhsT=tri_f, rhs=dA[:], start=True, stop=True), reads=["dA", "cst"], writes=["pcb"])
                    S.op("pe", lambda e: e.matmul(pcb[:, 16:32], lhsT=ones_f, rhs=dA[:], start=True, stop=True), reads=["dA", "cst"], writes=["pcb"])
                    S.op("act", lambda e: e.activation(out=ecs[b][:], in_=pcb[:, 0:16], func=AF.Exp), reads=["pcb"], writes=[("ecs", b)])
                    S.op("act", lambda e: e.activation(out=cdb[b][:], in_=pcb[:, 16:32], func=AF.Exp), reads=["pcb"], writes=[("cdb", b)])
                    yield
                    for g in range(4):
                        S.op("pe", lambda e, g=g: e.matmul(pcb[:, g * 128:(g + 1) * 128], lhsT=xc[:, 8 + g, c0:c0 + 128], rhs=xc[:, 12 + g, c0:c0 + 128],
                                                          start=True, stop=True),
                             reads=[("xc", 8 + g), ("xc", 12 + g)], writes=["pcb"])
                    S.op("dve", lambda e: e.tensor_tensor(out=cbm[:], in0=pcb[:].rearrange("p (g l) -> p g l", l=128),
                                                          in1=tri_f.unsqueeze(1).to_broadcast([128, 4, 128]), op=ALU.mult),
                         reads=["pcb", "cst"], writes=["cbm"])
                    for g in range(4):
                        S.op("dve", lambda e, g=g: e.tensor_tensor(out=MT[b][:, 4 * g:4 * g + 4, :], in0=dec[:, 4 * g:4 * g + 4, :],
                                                                    in1=cbm[:, g:g + 1, :].to_broadcast([128, 4, 128]), op=ALU.mult),
                             reads=[("dec", g), "cbm"], writes=[("MT", b, g)])
                    yield
                    S.op("pool", lambda e: e.tensor_tensor(out=xdte[b][:].rearrange("p (h d) -> p h d", d=64), in0=xdt[b][:].rearrange("p (h d) -> p h d", d=64),
                                                          in1=dec[:, :, 127:128].to_broadcast([128, 16, 64]), op=ALU.mult),
                         reads=[("xdt", b)] + [("dec", g) for g in range(4)], writes=[("xdte", b)])

                def backA(sl, ch):
                    c0 = ch * 128
                    tb = sl * 4 + ch
                    b = tb % 2
                    y1 = y1s[b]
                    for hf in range(2):
                        for hh in range(8):
                            h = hf * 8 + hh
                            S.op("pe", lambda e, h=h, hh=hh: e.matmul(pyd[:, hh * 64:(hh + 1) * 64], lhsT=MT[b][:, h, :], rhs=xdt[b][:, h * 64:(h + 1) * 64],
                                                                    start=True, stop=True),
                                 reads=[("MT", b, h // 4), ("xdt", b)], writes=["pyd"])
                        for gg in range(2):
                            g = hf * 2 + gg
                            S.op("pe", lambda e, g=g, gg=gg: e.matmul(pyo[:, gg * 256:(gg + 1) * 256], lhsT=xc[:, 12 + g, c0:c0 + 128],
                                                                    rhs=prevb[:, g * 256:(g + 1) * 256], start=True, stop=True),
                                 reads=[("xc", 12 + g), ("prevb", hf)], writes=["pyo"])
                        yield
                        hs = slice(hf * 512, (hf + 1) * 512)
                        S.op("dve", lambda e, hs=hs, hf=hf: e.tensor_tensor(out=y1[:, hs].rearrange("p (h d) -> p h d", d=64),
                                                                            in0=pyo[:].rearrange("p (h d) -> p h d", d=64),
                                                                            in1=ecs[b][:, hf * 8:hf * 8 + 8].unsqueeze(2).to_broadcast([128, 8, 64]), op=ALU.mult),
                             reads=["pyo", ("ecs", b)], writes=[("y1", b, hf)])
                        S.op("dve", lambda e, hs=hs, hf=hf: e.tensor_tensor(out=y1[:, hs], in0=y1[:, hs], in1=pyd[:], op=ALU.add),
                             reads=["pyd", ("y1", b, hf)], writes=[("y1", b, hf)])
                        yield
                        S.op("pool", lambda e, hs=hs, hf=hf: e.tensor_tensor(out=y2[:, hs].rearrange("p (h d) -> p h d", d=64),
                                                                             in0=xsT[b][:, hs].rearrange("p (h d) -> p h d", d=64),
                                                                             in1=dsk[:, hf * 8:hf * 8 + 8].unsqueeze(2).to_broadcast([128, 8, 64]), op=ALU.mult),
                             reads=[("xsT", b), "dsk"], writes=[("y2", hf)])
                        S.op("dve", lambda e, hs=hs, hf=hf: e.tensor_tensor(out=y1[:, hs], in0=y1[:, hs], in1=y2[:, hs], op=ALU.add),
                             reads=[("y1", b, hf), ("y2", hf)], writes=[("y1", b, hf)])
                        S.op("dve", lambda e, hs=hs, hf=hf: e.tensor_tensor(out=y1[:, hs], in0=y1[:, hs], in1=zsl[:, ch, hs], op=ALU.mult),
                             reads=[("y1", b, hf), ("zsl", ch)], writes=[("y1", b, hf)])
                        yield
                        for gg in range(2):
                            g = hf * 2 + gg
                            S.op("pe", lambda e, g=g, gg=gg: e.matmul(pyo[:, gg * 256:(gg + 1) * 256], lhsT=B_TM[b][:, g, :], rhs=xdte[b][:, g * 256:(g + 1) * 256],
                                                                    start=True, stop=True),
                                 reads=[("B_TM", b), ("xdte", b)], writes=["pyo"])
                        S.op("dve", lambda e, hs=hs, hf=hf: e.tensor_tensor(out=prev[:, hs].rearrange("p (h d) -> p h d", d=64),
                                                                            in0=prev[:, hs].rearrange("p (h d) -> p h d", d=64),
                                                                            in1=cdb[b][:, hf * 8:hf * 8 + 8].unsqueeze(2).to_broadcast([128, 8, 64]), op=ALU.mult),
                             reads=[("prev", hf), ("cdb", b)], writes=[("prev", hf)])
                        S.op("dve", lambda e, hs=hs, hf=hf: e.tensor_tensor(out=prev[:, hs], in0=prev[:, hs], in1=pyo[:], op=ALU.add),
                             reads=[("prev", hf), "pyo"], writes=[("prev", hf)])
                        S.op("act", lambda e, hs=hs, hf=hf: e.copy(out=prevb[:, hs], in_=prev[:, hs]), reads=[("prev", hf)], writes=[("prevb", hf)])

                def backB(sl, ch):
                    tb = sl * 4 + ch
                    b = tb % 2
                    y1 = y1s[b]
                    for g in range(4):
                        S.op("act", lambda e, g=g: e.activation(out=ysb[:, g * 256:(g + 1) * 256], in_=y1[:, g * 256:(g + 1) * 256], func=AF.Square,
                                                                accum_out=gss[:, g:g + 1]),
                             reads=[("y1", b, g // 2)], writes=["ysb", "gss"])
                    S.op("act", lambda e: e.activation(out=grs[:], in_=gss[:], func=AF.Ln, scale=1.0 / 256, bias=eps_t[:, 0:1]), reads=["gss", "eps"], writes=["grs"])
                    S.op("act", lambda e: e.activation(out=grs[:], in_=grs[:], func=AF.Exp, scale=-0.5), reads=["grs"], writes=["grs"])
                    for g in range(4):
                        S.op("act", lambda e, g=g: e.activation(out=ysb[:, g * 256:(g + 1) * 256], in_=y1[:, g * 256:(g + 1) * 256], func=AF.Copy, scale=grs[:, g:g + 1]),
                             reads=[("y1", b, g // 2), "grs"], writes=["ysb"])
                    yield
                    yi = tb % 2
                    for k in range(8):
                        S.op("pe", lambda e, k=k: e.transpose(out=ptr[:, k * 128:(k + 1) * 128], in_=ysb[:, k * 128:(k + 1) * 128], identity=ident_b[:]),
                             reads=["ysb", "ident_b"], writes=["ptr"])
                    S.op("act", lambda e: e.copy(out=ysT[yi][:].rearrange("p k t -> p (k t)"), in_=ptr[:]), reads=["ptr"], writes=[("ysT", yi)])
                    dma(ys_sp[:, :, tb * 128:(tb + 1) * 128].rearrange("k p t -> p k t"), ysT[yi][:], reads=[("ysT", yi)], writes=[("ys_sp", tb)])

                def run_interleaved(gens):
                    gens = [g for g in gens if g is not None]
                    while gens:
                        for g in list(gens):
                            try:
                                next(g)
                            except StopIteration:
                                gens.remove(g)

                pend_b = None
                for sl in range(NSLAB):
                    slab_front(sl)
                    run_interleaved([front(sl, 0), pend_b])
                    pend_b = None
                    for ch in range(4):
                        run_interleaved([front(sl, ch + 1) if ch < 3 else None, backA(sl, ch), pend_b])
                        pend_b = backB(sl, ch)
                run_interleaved([pend_b])
            S.barrier()

        if "D" in phases:
            with ExitStack() as pd:
                wst = [sb(f"d_wst{i}", [128, 8, 128], F32, pd) for i in range(2)]
                Wgs = sb("d_Wgs", [128, 8, D], BF16, pd)
                Wga = sb("d_Wga", [128, 8, D], BF16, pd)
                Wbs = sb("d_Wbs", [128, 8, D], BF16, pd)
                Wba = sb("d_Wba", [128, 8, D], BF16, pd)
                Wo = sb("d_Wo", [128, 8, D], BF16, pd)
                pgb = sb("d_pgb", [128, D], F32, pd)
                ysl = [sb(f"d_ysl{i}", [128, 8, 512], BF16, pd) for i in range(1)]
                yal = [sb(f"d_yal{i}", [128, 8, 512], BF16, pd) for i in range(1)]
                sgs = [sb(f"d_sgs{i}", [128, 512], F32, pd) for i in range(1)]
                sga = [sb(f"d_sga{i}", [128, 512], F32, pd) for i in range(1)]
                m1 = [sb(f"d_m1{i}", [128, 512], F32, pd) for i in range(1)]
                m2 = [sb(f"d_m2{i}", [128, 512], F32, pd) for i in range(1)]
                mT = sb("d_mT", [128, 8, 512], BF16, pd)
                xr = [sb(f"d_xr{i}", [128, D], F32, pd) for i in range(1)]
                ot = [sb(f"d_ot{i}", [128, D], F32, pd) for i in range(2)]
                oss = [sb(f"d_oss{i}", [128, 2], F32, pd) for i in range(2)]
                ors = [sb(f"d_ors{i}", [128, 1], F32, pd) for i in range(2)]
                pps = ps("d_pps", [128, 512], F32, pd)
                ppa = ps("d_ppa", [128, 512], F32, pd)
                pgs = ps("d_pgs", [128, 512], F32, pd)
                pga = ps("d_pga", [128, 512], F32, pd)
                pout = [ps(f"d_pout{i}", [128, 2, 512], F32, pd) for i in range(2)]

                def load_slab(sl):
                    t0 = sl * 512
                    dma(ysl[0][:], ys_sp[:, :, t0:t0 + 512].rearrange("k p t -> p k t"),
                        reads=[("ys_sp", tb) for tb in range(sl * 4, sl * 4 + 4)], writes=[("ysl", 0)])
                    dma(yal[0][:], ya_sp[:, :, t0:t0 + 512].rearrange("k p t -> p k t"),
                        reads=[("ya_sp", hp) for hp in range(8)], writes=[("yal", 0)])

                load_slab(0)
                dma(pgb[:], post_g.partition_broadcast(128), writes=["pgb"])
                def load_dc_weights(c4):
                    cs_ = slice(c4 * 128, (c4 + 1) * 128)
                    load_weight(Wgs[:, :, cs_], ("Wgs", c4), [(w_in[:, O_GS + c4 * 128:O_GS + (c4 + 1) * 128], 0)], wst, "d_wst", gT, "gT", eng="dve")
                    load_weight(Wga[:, :, cs_], ("Wga", c4), [(w_in[:, O_GA + c4 * 128:O_GA + (c4 + 1) * 128], 0)], wst, "d_wst", gT, "gT", eng="dve")
                    load_weight(Wbs[:, :, cs_], ("Wbs", c4), [(w_bs[:, cs_], 0)], wst, "d_wst", ngT, "ngT", eng="dve")
                    load_weight(Wba[:, :, cs_], ("Wba", c4), [(w_ba[:, cs_], 0)], wst, "d_wst", None, None, eng="dve")

                def load_wo(c4):
                    cs_ = slice(c4 * 128, (c4 + 1) * 128)
                    load_weight(Wo[:, :, cs_], ("Wo", c4), [(w_o[:, cs_], 0)], wst, "d_wst", None, None, eng="dve")

                load_dc_weights(0)
                nd = [0]
                for sl in range(NSLAB):
                    t0 = sl * 512
                    li = 0
                    hk = hT_keys(t0, t0 + 512)
                    for dc in range(8):
                        i = 0
                        dsl = slice(dc * 128, (dc + 1) * 128)
                        if sl == 0:
                            if dc < 7:
                                load_dc_weights(dc + 1)
                            load_wo(dc)
                        for k in range(8):
                            S.op("pe", lambda e, k=k, dsl=dsl, t0=t0: e.matmul(pgs[:], lhsT=Wgs[:, k, dsl], rhs=hT[:, k, t0:t0 + 512], start=(k == 0), stop=(k == 7)),
                                 reads=hk + [("Wgs", dc)], writes=["pgs"])
                        for k in range(8):
                            S.op("pe", lambda e, k=k, dsl=dsl, t0=t0: e.matmul(pga[:], lhsT=Wga[:, k, dsl], rhs=hT[:, k, t0:t0 + 512], start=(k == 0), stop=(k == 7)),
                                 reads=hk + [("Wga", dc)], writes=["pga"])
                        for k in range(8):
                            S.op("pe", lambda e, k=k, dsl=dsl, li=li: e.matmul(pps[:], lhsT=Wbs[:, k, dsl], rhs=ysl[li][:, k, :], start=(k == 0), stop=(k == 7)),
                                 reads=[("ysl", li), ("Wbs", dc)], writes=["pps"])
                        for k in range(8):
                            S.op("pe", lambda e, k=k, dsl=dsl, li=li: e.matmul(ppa[:], lhsT=Wba[:, k, dsl], rhs=yal[li][:, k, :], start=(k == 0), stop=(k == 7)),
                                 reads=[("yal", li), ("Wba", dc)], writes=["ppa"])
                        S.op("act", lambda e, i=i: e.activation(out=sgs[i][:], in_=pgs[:], func=AF.Sigmoid), reads=["pgs"], writes=[("sgs", i)])
                        S.op("act", lambda e, i=i: e.activation(out=sga[i][:], in_=pga[:], func=AF.Sigmoid), reads=["pga"], writes=[("sga", i)])
                        S.op("dve", lambda e, i=i: e.tensor_tensor(out=m1[i][:], in0=pps[:], in1=sgs[i][:], op=ALU.mult), reads=["pps", ("sgs", i)], writes=[("m1", i)])
                        S.op("dve", lambda e, i=i: e.tensor_tensor(out=m2[i][:], in0=ppa[:], in1=sga[i][:], op=ALU.mult), reads=["ppa", ("sga", i)], writes=[("m2", i)])
                        S.op("pool", lambda e, i=i, dc=dc: e.tensor_tensor(out=mT[:, dc, :], in0=m1[i][:], in1=m2[i][:], op=ALU.add),
                             reads=[("m1", i), ("m2", i)], writes=[("mT", dc)])
                    if sl < NSLAB - 1:
                        load_slab(sl + 1)
                    for ch in range(4):
                        tb = sl * 4 + ch
                        i = tb % 2
                        dma(xr[0][:], x_d[tb * 128:(tb + 1) * 128, :], writes=[("xr", 0)])
                        for hf in range(2):
                            for k in range(8):
                                S.op("pe", lambda e, i=i, k=k, hf=hf, ch=ch: e.matmul(pout[i][:, hf, :], lhsT=mT[:, k, ch * 128:(ch + 1) * 128], rhs=Wo[:, k, hf * 512:(hf + 1) * 512],
                                                                                     start=(k == 0), stop=(k == 7)),
                                     reads=[("mT", k)] + [("Wo", 4 * hf + j_) for j_ in range(4)], writes=[("pout", i)])
                        for hf in range(2):
                            S.op("act", lambda e, i=i, hf=hf: e.activation(out=ot[i][:, hf * 512:(hf + 1) * 512], in_=pout[i][:, hf, :], func=AF.Square,
                                                                           accum_out=oss[i][:, hf:hf + 1]),
                                 reads=[("pout", i)], writes=[("ot", i), ("oss", i)])
                        S.op("dve", lambda e, i=i: e.tensor_tensor(out=ors[i][:], in0=oss[i][:, 0:1], in1=oss[i][:, 1:2], op=ALU.add), reads=[("oss", i)], writes=[("ors", i)])
                        S.op("act", lambda e, i=i: e.activation(out=ors[i][:], in_=ors[i][:], func=AF.Ln, scale=1.0 / D, bias=eps_t[:, 0:1]), reads=[("ors", i), "eps"], writes=[("ors", i)])
                        S.op("act", lambda e, i=i: e.activation(out=ors[i][:], in_=ors[i][:], func=AF.Exp, scale=-0.5), reads=[("ors", i)], writes=[("ors", i)])
                        S.op("dve", lambda e, i=i: e.scalar_tensor_tensor(out=ot[i][:], in0=pout[i][:].rearrange("p a b -> p (a b)"), scalar=ors[i][:, 0:1], in1=pgb[:],
                                                                          op0=ALU.mult, op1=ALU.mult),
                             reads=[("pout", i), ("ors", i), "pgb"], writes=[("ot", i)])
                        S.op("pool", lambda e, i=i: e.tensor_tensor(out=ot[i][:], in0=ot[i][:], in1=xr[0][:], op=ALU.add), reads=[("ot", i), ("xr", 0)], writes=[("ot", i)])
                        dma(out_d[tb * 128:(tb + 1) * 128, :], ot[i][:], reads=[("ot", i)])
        S.emit(sems, dmasems)
    return nc


def _consts():
    c = np.zeros((128, 4, 128), np.float32)
    j = np.arange(128)
    c[:, 0, :] = np.eye(128, dtype=np.float32)
    c[:, 1, :] = (j[:, None] <= j[None, :]).astype(np.float32)
    c[:, 2, :] = (j[:, None] > j[None, :]).astype(np.float32)
    c[:, 3, :] = 1.0
    sel = np.zeros((16, 16, 128), np.float32)
    for h in range(16):
        sel[h, h, :] = 1.0
    return c, sel


def make_in_maps(inputs):
    c, sel = _consts()
    f = lambda a: np.ascontiguousarray(np.asarray(a, dtype=np.float32))
    shared = {
        "w_in": f(inputs["w_in"][0]), "w_bs": f(inputs["w_branch_ssm"][0]), "w_ba": f(inputs["w_branch_att"][0]),
        "w_o": f(inputs["w_out"][0]), "pre_g": f(np.asarray(inputs["pre_norm_g"][0]).reshape(8, 128).T), "post_g": f(inputs["post_norm_g"][0]),
        "norm_g": f(np.asarray(inputs["ssm_norm_g"][0]).reshape(8, 128).T), "conv_w": f(np.asarray(inputs["conv_w"][0]).reshape(4, 16, 128).transpose(2, 1, 0)), "conv_b": f(np.asarray(inputs["conv_b"][0]).reshape(16, 128).T),
        "dt_bias": f(inputs["dt_bias"][0]), "a_log": f(inputs["a_log"][0]), "d_skip": f(inputs["d_skip"][0]),
        "fgate_b": f(inputs["fgate_b"][0]), "consts": c, "sel": sel,
    }
    x = np.asarray(inputs["x"], dtype=np.float32)
    return [dict(shared, x=np.ascontiguousarray(x[b])) for b in range(x.shape[0])]


def kernel(**inputs):
    in_maps = make_in_maps(inputs)
    nc = build_nc()
    res = run_bass_kernel_spmd(nc, in_maps, core_ids=list(range(8)))
    return np.stack([np.asarray(r["out"], dtype=np.float32) for r in res.results], axis=0)
```

```python
import numpy as np
from contextlib import ExitStack
import concourse.bass as bass
import concourse.mybir as mybir
from concourse.bass_utils import run_bass_kernel_spmd

F32 = mybir.dt.float32
BF16 = mybir.dt.bfloat16
ALU = mybir.AluOpType
AF = mybir.ActivationFunctionType

ENGS = ("pe", "act", "dve", "pool", "sp")
NDMASEM = 8
NPP = 6
PSUM_NAMES = {"pt", "pg", "pp", "pst", "pO", "ppx", "pseg", "pcb", "pyd", "pyo", "ptr", "pps", "ppa", "pgs", "pga", "pout"}

L = 4096
D = 1024
NTB = 32
NSLAB = 8
O_ZS, O_XBC, O_DT, O_Q, O_K, O_V, O_F, O_ZA, O_GS, O_GA = 0, 1024, 3072, 3088, 4112, 5136, 6160, 6176, 7200, 8224
NCOLS = 9248
EPS = 1e-6


class _Op:
    __slots__ = ("eng", "fn", "deps", "is_dma", "sig", "dma_slot", "dma_val", "idx", "has_dep")

    def __init__(self, eng, fn, is_dma):
        self.eng = eng
        self.fn = fn
        self.deps = []
        self.is_dma = is_dma
        self.sig = None
        self.has_dep = False
        self.dma_slot = None
        self.dma_val = None
        self.idx = None


class Sched:
    def __init__(self, nc):
        self.nc = nc
        self.ops = {e: [] for e in ENGS}
        self.dmaops = {e: [] for e in ENGS}
        self.last_w = {}
        self.readers = {}
        self.pending = {e: [] for e in ENGS}

    def op(self, eng, fn, reads=(), writes=(), dma=False):
        rd, wr = [], list(writes)
        for r in reads:
            nm = r[0] if isinstance(r, tuple) else r
            if nm in PSUM_NAMES:
                wr.append(r)
            else:
                rd.append(r)
        reads, writes = rd, wr
        o = _Op(eng, fn, dma)
        o.idx = len(self.ops[eng])
        deps = list(self.pending[eng])
        self.pending[eng] = []
        for r in reads:
            w = self.last_w.get(r)
            if w is not None:
                deps.append(w)
        for wkey in writes:
            w = self.last_w.get(wkey)
            if w is not None:
                deps.append(w)
            deps.extend(self.readers.get(wkey, ()))
        seen = set()
        for d in deps:
            if d is o or id(d) in seen:
                continue
            seen.add(id(d))
            if not d.is_dma and d.eng == eng:
                if eng in ("pe", "sp"):
                    continue
            o.deps.append(d)
            d.has_dep = True
        if dma:
            n = len(self.dmaops[eng])
            o.dma_slot = n % NDMASEM
            o.dma_val = 16 * (n // NDMASEM + 1)
            if n >= NDMASEM:
                o.deps.append(self.dmaops[eng][n - NDMASEM])
            self.dmaops[eng].append(o)
        self.ops[eng].append(o)
        for r in reads:
            self.readers.setdefault(r, []).append(o)
        for wkey in writes:
            self.last_w[wkey] = o
            self.readers[wkey] = []
        return o

    def barrier(self):
        lasts = []
        for e in ENGS:
            nd = [x for x in self.ops[e] if not x.is_dma]
            if nd:
                lasts.append(nd[-1])
            lasts.extend(self.dmaops[e][-NDMASEM:])
        for e in ENGS:
            self.pending[e] = list(lasts)

    def emit(self, sems, dmasems):
        nc = self.nc
        for e in ENGS:
            c = 0
            for o in self.ops[e]:
                if o.is_dma:
                    continue
                if o.has_dep:
                    c += 1
                    o.sig = c

        def mk(e):
            def body(eng):
                waited = {}
                for o in self.ops[e]:
                    for d in o.deps:
                        if d.is_dma:
                            key = ("d", d.eng, d.dma_slot)
                            val = d.dma_val
                            sem = dmasems[d.eng][d.dma_slot]
                        else:
                            key = ("c", d.eng)
                            val = d.sig
                            sem = sems[d.eng]
                        if waited.get(key, 0) >= val:
                            continue
                        waited[key] = val
                        eng.wait_ge(sem, val)
                    ins = o.fn(eng)
                    if o.is_dma:
                        ins.then_inc(dmasems[e][o.dma_slot], 16)
                    elif o.has_dep:
                        ins.then_inc(sems[e], 1)
                lastvals = {}
                for x in self.dmaops[e]:
                    lastvals[x.dma_slot] = x.dma_val
                for s, v in lastvals.items():
                    eng.wait_ge(dmasems[e][s], v)
            return body

        with nc.Block() as block:
            block.tensor(mk("pe"))
            block.scalar(mk("act"))
            block.vector(mk("dve"))
            block.gpsimd(mk("pool"))
            block.sync(mk("sp"))


def build_nc(phases="ABCD", debug=False):
    nc = bass.Bass("TRN2", target_bir_lowering=False)
    dkind = "ExternalOutput" if debug else None

    def dram(name, shape, dt, kind=None):
        if kind is None:
            return nc.dram_tensor(name, shape, dt).ap()
        return nc.dram_tensor(name, shape, dt, kind=kind).ap()

    x_d = dram("x", [L, D], F32, "ExternalInput")
    w_in = dram("w_in", [D, NCOLS], F32, "ExternalInput")
    w_bs = dram("w_bs", [D, D], F32, "ExternalInput")
    w_ba = dram("w_ba", [D, D], F32, "ExternalInput")
    w_o = dram("w_o", [D, D], F32, "ExternalInput")
    pre_g = dram("pre_g", [128, 8], F32, "ExternalInput")
    post_g = dram("post_g", [D], F32, "ExternalInput")
    norm_g = dram("norm_g", [128, 8], F32, "ExternalInput")
    conv_w = dram("conv_w", [128, 16, 4], F32, "ExternalInput")
    conv_b = dram("conv_b", [128, 16], F32, "ExternalInput")
    dt_bias = dram("dt_bias", [16], F32, "ExternalInput")
    a_log = dram("a_log", [16], F32, "ExternalInput")
    d_skip = dram("d_skip", [16], F32, "ExternalInput")
    fgate_b = dram("fgate_b", [16], F32, "ExternalInput")
    consts = dram("consts", [128, 4, 128], F32, "ExternalInput")
    sel_d = dram("sel", [16, 16, 128], F32, "ExternalInput")
    out_d = dram("out", [L, D], F32, "ExternalOutput")
    ya_sp = dram("ya_sp", [8, 128, L], BF16, dkind)
    ys_sp = dram("ys_sp", [8, 128, L], BF16, dkind)

    dbg_g = dram("dbg_g", [128, 4, 512], F32, "ExternalOutput") if debug else None
    S = Sched(nc)
    with ExitStack() as es:
        def sb(name, shape, dt, ctx=None):
            return (ctx or es).enter_context(nc.sbuf_tensor(name, shape, dt))

        def ps(name, shape, dt, ctx=None):
            return (ctx or es).enter_context(nc.psum_tensor(name, shape, dt))

        sems = {e: es.enter_context(nc.semaphore("s_" + e)) for e in ENGS}
        dmasems = {e: [es.enter_context(nc.semaphore(f"d_{e}{i}")) for i in range(NDMASEM)] for e in ENGS}

        def dma(out, in_, reads=(), writes=(), q="sp", **kw):
            return S.op(q, lambda e: e.dma_start(out=out, in_=in_, **kw), reads=reads, writes=writes, dma=True)

        hT = sb("hT", [128, 8, L], BF16)
        cst = sb("cst", [128, 4, 128], F32)
        ident_b = sb("ident_b", [128, 128], BF16)
        tri_b = sb("tri_b", [128, 128], BF16)
        gT = sb("gT", [128, 8], F32)
        ngT = sb("ngT", [128, 8], F32)
        eps_t = sb("eps_t", [128, 1], F32)
        ident_f = cst[:, 0, :]
        tri_f = cst[:, 1, :]
        sgt_f = cst[:, 2, :]
        ones_f = cst[:, 3, :]

        dma(cst[:], consts[:, :, :], writes=["cst"])
        dma(gT[:], pre_g[:, :], writes=["gT"])
        dma(ngT[:], norm_g[:, :], writes=["ngT"])
        S.op("pool", lambda e: e.memset(eps_t[:], EPS), writes=["eps"])
        S.op("pool", lambda e: e.tensor_copy(out=ident_b[:], in_=ident_f), reads=["cst"], writes=["ident_b"])
        S.op("pool", lambda e: e.tensor_copy(out=tri_b[:], in_=tri_f), reads=["cst"], writes=["tri_b"])

        wcount = [0]
        def load_weight(dst, dst_key, src_slices, stg, stg_key, gain, gain_key, eng="pool", defer=False):
            if isinstance(stg, list):
                j = wcount[0] % len(stg)
                wcount[0] += 1
                stg, stg_key = stg[j], (stg_key, j)
            n_tot = 0
            for (src, off) in src_slices:
                n = src.shape[1]
                dma(stg[:, :, off:off + n], src.rearrange("(k p) n -> p k n", p=128), writes=[stg_key])
                n_tot = max(n_tot, off + n)
            casts = []
            for k in range(8):
                def cast(k=k):
                    if gain is not None:
                        S.op(eng, lambda e: e.tensor_scalar(out=dst[:, k, 0:n_tot], in0=stg[:, k, 0:n_tot], scalar1=gain[:, k:k + 1],
                                                            scalar2=None, op0=ALU.mult),
                             reads=[stg_key, gain_key], writes=[dst_key])
                    else:
                        S.op(eng, lambda e: e.tensor_copy(out=dst[:, k, 0:n_tot], in_=stg[:, k, 0:n_tot]),
                             reads=[stg_key], writes=[dst_key])
                casts.append(cast)
            if defer:
                return casts
            for c in casts:
                c()
            return []

        with ExitStack() as pa:
            xt = [sb(f"xt{i}", [128, D], F32, pa) for i in range(3)]
            sq = sb("sq", [128, D], F32, pa)
            ss = [sb(f"ss{i}", [128, 1], F32, pa) for i in range(3)]
            rs = [sb(f"rs{i}", [128, 1], F32, pa) for i in range(3)]
            hb = [sb(f"hb{i}", [128, D], BF16, pa) for i in range(3)]
            pt = [ps(f"pt{i}", [128, 8, 128], BF16, pa) for i in range(3)]
            for tb in range(NTB):
                i = tb % 3
                dma(xt[i][:], x_d[tb * 128:(tb + 1) * 128, :], writes=[("xt", i)])
                S.op("act", lambda e, i=i: e.activation(out=sq[:], in_=xt[i][:], func=AF.Square, accum_out=ss[i][:]),
                     reads=[("xt", i)], writes=["sq", ("ss", i)])
                S.op("act", lambda e, i=i: e.activation(out=rs[i][:], in_=ss[i][:], func=AF.Ln, scale=1.0 / D, bias=eps_t[:, 0:1]),
                     reads=[("ss", i), "eps"], writes=[("rs", i)])
                S.op("act", lambda e, i=i: e.activation(out=rs[i][:], in_=rs[i][:], func=AF.Exp, scale=-0.5),
                     reads=[("rs", i)], writes=[("rs", i)])
                S.op("dve", lambda e, i=i: e.tensor_scalar(out=hb[i][:], in0=xt[i][:], scalar1=rs[i][:, 0:1], scalar2=None, op0=ALU.mult),
                     reads=[("rs", i), ("xt", i)], writes=[("hb", i)])
                for k in range(8):
                    S.op("pe", lambda e, i=i, k=k: e.transpose(out=pt[i][:, k, :], in_=hb[i][:, k * 128:(k + 1) * 128], identity=ident_b[:]),
                         reads=[("hb", i), "ident_b"], writes=[("pt", i)])
                S.op("pool" if False else "dve", lambda e, i=i, tb=tb: e.tensor_copy(out=hT[:, :, tb * 128:(tb + 1) * 128], in_=pt[i][:]),
                     reads=[("pt", i)], writes=[("hT", tb)])
        S.barrier()

        def hT_keys(t0, t1):
            return [("hT", tb) for tb in range(t0 // 128, (t1 + 127) // 128)]

        if "B" in phases:
            with ExitStack() as pb:
                wst = sb("b_wst", [128, 8, 512], F32, pb)
                wb = sb("b_wb", [128, 8, 512], BF16, pb)
                wfs = sb("b_wfs", [128, 8, 16], F32, pb)
                wfb = sb("b_wfb", [128, 8, 16], BF16, pb)
                qT = sb("b_qT", [128, L], BF16, pb)
                kT = sb("b_kT", [128, L], BF16, pb)
                v1 = sb("b_v1", [128, NTB, 2, 65], BF16, pb)
                zs = sb("b_zs", [128, NTB, 128], BF16, pb)
                yat = sb("b_yat", [128, NTB, 128], BF16, pb)
                yTs = sb("b_yTs", [128, L], BF16, pb)
                T1 = [sb(f"b_T1{i}", [128, 512], F32, pb) for i in range(6)]
                PT = [sb(f"b_PT{i}", [128, 512], BF16, pb) for i in range(8)]
                Ftb = [sb(f"b_Ftb{i}", [128, 512], F32, pb) for i in range(4)]
                gend = [sb(f"b_gend{i}", [128, 1], F32, pb) for i in range(4)]
                kbias = [sb(f"b_kbias{i}", [128, NTB], F32, pb) for i in range(4)]
                rden = [sb(f"b_rden{i}", [128, 4, 1], F32, pb) for i in range(2)]
                otmp = [sb(f"b_otmp{i}", [128, 4, 64], F32, pb) for i in range(2)]
                fgb = sb("b_fgb", [128, 16], F32, pb)
                zt = sb("b_zt", [128, 260], BF16, pb)
                spf = sb("b_spf", [128, NTB, 16], F32, pb)
                tot = sb("b_tot", [128, NTB, 16], F32, pb)
                pre = sb("b_pre", [128, NTB, 16], F32, pb)
                G_TM = sb("b_GTM", [128, NTB, 16], F32, pb)
                G_FM = sb("b_GFM", [16, L], F32, pb)
                sel = sb("b_sel", [16, 16, 128], F32, pb)
                pp = [ps(f"b_pp{i}", [128, 512], F32, pb) for i in range(NPP)]
                pst = pp
                pO = [ps(f"b_pO{i}", [128, 512], F32, pb) for i in range(2)]
                pg = pp[2]

                dma(fgb[:], fgate_b.partition_broadcast(128), writes=["fgb"])
                dma(sel[:], sel_d[:, :, :], writes=["sel"])
                S.op("pool", lambda e: e.memset(v1[:, :, :, 64:65], 1.0), writes=["v1ones"])
                S.op("pool", lambda e: e.memset(zt[:], 0.0), writes=["zt"])

                load_weight(wfb, "wfb", [(w_in[:, O_F:O_F + 16], 0)], wfs, "wfs", gT, "gT")
                for tb in range(NTB):
                    for k in range(8):
                        S.op("pe", lambda e, tb=tb, k=k: e.matmul(pg[:, tb * 16:(tb + 1) * 16], lhsT=hT[:, k, tb * 128:(tb + 1) * 128],
                                                               rhs=wfb[:, k, :], start=(k == 0), stop=(k == 7)),
                             reads=[("hT", tb), "wfb"], writes=[("pp", 2)])
                S.op("dve", lambda e: e.tensor_tensor(out=spf[:], in0=pg[:].rearrange("p (a b) -> p a b", b=16),
                                                      in1=fgb[:].unsqueeze(1).to_broadcast([128, NTB, 16]), op=ALU.add),
                     reads=[("pp", 2), "fgb"], writes=["spf"])
                S.op("act", lambda e: e.activation(out=spf[:], in_=spf[:], func=AF.Exp, scale=-1.0), reads=["spf"], writes=["spf"])
                S.op("act", lambda e: e.activation(out=spf[:], in_=spf[:], func=AF.Ln, bias=1.0), reads=["spf"], writes=["spf"])
                npp = [0]
                nst = [0]
                nT1 = [0]
                nPT = [0]
                nO = [0]
                nF = [0]
                def load_pair_weights(hp, defer=False):
                    return load_weight(wb, "b_wb", [(w_in[:, O_Q + hp * 128:O_Q + (hp + 1) * 128], 0),
                                                    (w_in[:, O_K + hp * 128:O_K + (hp + 1) * 128], 128),
                                                    (w_in[:, O_V + hp * 128:O_V + (hp + 1) * 128], 256),
                                                    (w_in[:, O_ZA + hp * 128:O_ZA + (hp + 1) * 128], 384)], wst, "b_wst", gT, "gT", eng="dve", defer=defer)

                def proj_qk(hp):
                    for (dst, dkey, c0) in ((qT, "qT", 0), (kT, "kT", 128)):
                        for sl in range(NSLAB):
                            i = npp[0] % NPP
                            npp[0] += 1
                            for k in range(8):
                                S.op("pe", lambda e, i=i, k=k, sl=sl, c0=c0: e.matmul(pp[i][:], lhsT=wb[:, k, c0:c0 + 128], rhs=hT[:, k, sl * 512:(sl + 1) * 512],
                                                                                    start=(k == 0), stop=(k == 7)),
                                     reads=hT_keys(sl * 512, (sl + 1) * 512) + ["b_wb"], writes=[("pp", i)])
                            S.op("act" if sl % 2 == 0 else "dve",
                                 (lambda e, i=i, sl=sl, dst=dst: e.copy(out=dst[:, sl * 512:(sl + 1) * 512], in_=pp[i][:])) if sl % 2 == 0 else
                                 (lambda e, i=i, sl=sl, dst=dst: e.tensor_copy(out=dst[:, sl * 512:(sl + 1) * 512], in_=pp[i][:])),
                                 reads=[("pp", i)], writes=[(dkey, sl)])

                def proj_vz(hp):
                    for tb in range(NTB):
                        i = npp[0] % NPP
                        npp[0] += 1
                        for k in range(8):
                            S.op("pe", lambda e, i=i, k=k, tb=tb: e.matmul(pp[i][:, 0:256], lhsT=hT[:, k, tb * 128:(tb + 1) * 128], rhs=wb[:, k, 256:512],
                                                                         start=(k == 0), stop=(k == 7)),
                                 reads=[("hT", tb), "b_wb"], writes=[("pp", i)])
                        S.op("dve", lambda e, i=i, tb=tb: e.tensor_copy(out=v1[:, tb, :, 0:64], in_=pp[i][:, 0:128].rearrange("p (a b) -> p a b", b=64)),
                             reads=[("pp", i)], writes=[("v1", tb)])
                        S.op("act", lambda e, i=i, tb=tb: e.activation(out=zs[:, tb, :], in_=pp[i][:, 128:256], func=AF.Silu),
                             reads=[("pp", i)], writes=[("zs", tb)])

                def spill_pair(hp):
                    for sp_ in range(NSLAB):
                        i = npp[0] % NPP
                        npp[0] += 1
                        ptb = pp[i][:].bitcast(BF16)
                        for j in range(4):
                            tb = 4 * sp_ + j
                            S.op("pe", lambda e, ptb=ptb, j=j, tb=tb: e.transpose(out=ptb[:, j * 128:(j + 1) * 128], in_=yat[:, tb, :], identity=ident_b[:]),
                                 reads=[("yat", sp_), "ident_b"], writes=[("pp", i)])
                        S.op("dve", lambda e, ptb=ptb, sp_=sp_: e.tensor_copy(out=yTs[:, sp_ * 512:(sp_ + 1) * 512], in_=ptb[:, 0:512]),
                             reads=[("pp", i)], writes=[("yTs", sp_)])
                    dma(ya_sp[hp, :, :], yTs[:], reads=[("yTs", s_) for s_ in range(NSLAB)], writes=[("ya_sp", hp)])

                def forget_rest():
                    spf2 = spf[:].rearrange("p a b -> p (a b)")
                    S.op("pe", lambda e: e.matmul(pp[0][:], lhsT=tri_f, rhs=spf2, start=True, stop=True), reads=["spf", "cst"], writes=[("pp", 0)])
                    S.op("pe", lambda e: e.matmul(pp[1][:], lhsT=ones_f, rhs=spf2, start=True, stop=True), reads=["spf", "cst"], writes=[("pp", 1)])
                    S.op("dve", lambda e: e.tensor_copy(out=tot[:].rearrange("p a b -> p (a b)"), in_=pp[1][:]), reads=[("pp", 1)], writes=["tot"])
                    S.op("dve", lambda e: e.memset(pre[:, 0, :], 0.0), writes=["pre"])
                    for b in range(1, NTB):
                        S.op("dve", lambda e, b=b: e.tensor_tensor(out=pre[:, b, :], in0=pre[:, b - 1, :], in1=tot[:, b - 1, :], op=ALU.add),
                             reads=["pre", "tot"], writes=["pre"])
                    S.op("dve", lambda e: e.tensor_tensor(out=G_TM[:].rearrange("p a b -> p (a b)"), in0=pp[0][:],
                                                          in1=pre[:].rearrange("p a b -> p (a b)"), op=ALU.add),
                         reads=[("pp", 0), "pre"], writes=["G_TM"])
                    if debug:
                        dma(dbg_g[:, 0, :], spf[:].rearrange("p a b -> p (a b)"), reads=["spf"])
                        dma(dbg_g[:, 1, :], tot[:].rearrange("p a b -> p (a b)"), reads=["tot"])
                        dma(dbg_g[:, 2, :], pre[:].rearrange("p a b -> p (a b)"), reads=["pre"])
                        dma(dbg_g[:, 3, :], G_TM[:].rearrange("p a b -> p (a b)"), reads=["G_TM"])
                    pgT = pp[0][0:16, :]
                    for g4 in range(8):
                        for j in range(4):
                            tb = g4 * 4 + j
                            S.op("pe", lambda e, tb=tb, j=j: e.transpose(out=pgT[:, j * 128:(j + 1) * 128], in_=G_TM[:, tb, :], identity=ident_f),
                                 reads=["G_TM", "cst"], writes=[("pp", 0)])
                        S.op("dve", lambda e, g4=g4: e.tensor_copy(out=G_FM[:, g4 * 512:(g4 + 1) * 512], in_=pgT), reads=[("pp", 0)], writes=["G_FM"])


                load_pair_weights(0)
                proj_qk(0)
                proj_vz(0)
                forget_rest()
                for hp in range(8):
                    pending_casts = load_pair_weights(hp + 1, defer=True) if hp < 7 else []
                    LA = 2
                    jobs = []
                    for sp_ in range(NSLAB):
                        for kb in range(4 * sp_ + 4):
                            jobs.append((sp_, kb))
                    st = {}

                    def stage1(job):
                        sp_, kb = job
                        nkb = 4 * sp_ + 4
                        if kb == 0:
                            for hl in range(2):
                                h = hp * 2 + hl
                                fi = (sp_ % 2) * 2 + hl
                                gi = nst[0] % NPP
                                nst[0] += 1
                                S.op("pe", lambda e, h=h, gi=gi: e.matmul(pp[gi][:], lhsT=sel[:, h, :], rhs=G_FM[:, sp_ * 512:(sp_ + 1) * 512], start=True, stop=True),
                                     reads=["sel", "G_FM"], writes=[("pp", gi)])
                                S.op("act", lambda e, fi=fi, gi=gi: e.copy(out=gend[fi][:], in_=pp[gi][:, 511:512]), reads=[("pp", gi)], writes=[("gend", fi)])
                                S.op("dve", lambda e, fi=fi, gi=gi: e.tensor_scalar(out=Ftb[fi][:], in0=pp[gi][:], scalar1=gend[fi][:, 0:1], scalar2=-1.0,
                                                                                    op0=ALU.subtract, op1=ALU.mult),
                                     reads=[("pp", gi), ("gend", fi)], writes=[("Ftb", fi)])
                                S.op("dve", lambda e, fi=fi, h=h: e.tensor_scalar(out=kbias[fi][:, 0:nkb], in0=G_TM[:, 0:nkb, h], scalar1=gend[fi][:, 0:1],
                                                                                  scalar2=None, op0=ALU.subtract),
                                     reads=["G_TM", ("gend", fi)], writes=[("kbias", fi)])
                        m = kb - 4 * sp_
                        c0 = max(m, 0) * 128
                        info = []
                        for hl in range(2):
                            si = nst[0] % NPP
                            nst[0] += 1
                            ti = nT1[0] % 6
                            nT1[0] += 1
                            pi = nPT[0] % 8
                            nPT[0] += 1
                            info.append((si, ti, pi))
                        st[job] = info
                        for hl in range(2):
                            p0 = hl * 64
                            si = info[hl][0]
                            S.op("pe", lambda e, p0=p0, si=si: e.matmul(
                                pst[si][:, c0:512], lhsT=kT[p0:p0 + 64, kb * 128:(kb + 1) * 128],
                                rhs=qT[p0:p0 + 64, sp_ * 512 + c0:(sp_ + 1) * 512], start=True, stop=True),
                                reads=[("kT", kb // 4), ("qT", sp_)], writes=[("pp", si)])
                        for hl in range(2):
                            fi = (sp_ % 2) * 2 + hl
                            si, ti, pi = info[hl]
                            S.op("dve", lambda e, si=si, ti=ti, fi=fi: e.scalar_tensor_tensor(
                                out=T1[ti][:, c0:512], in0=pst[si][:, c0:512], scalar=0.125, in1=Ftb[fi][:, c0:512], op0=ALU.mult, op1=ALU.add),
                                reads=[("pp", si), ("Ftb", fi)], writes=[("T1", ti)])
                            S.op("act", lambda e, ti=ti, pi=pi, fi=fi: e.activation(
                                out=PT[pi][:, c0:512], in_=T1[ti][:, c0:512], func=AF.Exp, bias=kbias[fi][:, kb:kb + 1]),
                                reads=[("T1", ti), ("kbias", fi)], writes=[("PT", pi)])
                            if m >= 0:
                                S.op("pool", lambda e, pi=pi: e.tensor_tensor(out=PT[pi][:, c0:c0 + 128], in0=PT[pi][:, c0:c0 + 128],
                                                                              in1=tri_b[:], op=ALU.mult),
                                     reads=[("PT", pi), "tri_b"], writes=[("PT", pi)])

                    def stage2(job):
                        sp_, kb = job
                        m = kb - 4 * sp_
                        for hl in range(2):
                            p0 = hl * 64
                            oi = hl
                            pi = st[job][hl][2]
                            if kb == 0:
                                S.op("pe", lambda e, oi=oi: e.matmul(pO[oi][:, 0:260], lhsT=zt[:, 0:128], rhs=zt[:, 0:260],
                                                                     start=True, stop=False),
                                     reads=["zt"], writes=[("pO", oi)])
                            for ql in range(max(m, 0), 4):
                                S.op("pe", lambda e, ql=ql, oi=oi, pi=pi, hl=hl: e.matmul(
                                    pO[oi][:, ql * 65:(ql + 1) * 65], lhsT=PT[pi][:, ql * 128:(ql + 1) * 128], rhs=v1[:, kb, hl, :],
                                    start=False, stop=(kb == 4 * sp_ + 3 and ql == 3)),
                                    reads=[("PT", pi), ("v1", kb), "v1ones"], writes=[("pO", oi)])
                            if kb == 4 * sp_ + 3:
                                S.op("dve", lambda e, oi=oi: e.reciprocal(out=rden[oi][:], in_=pO[oi][:, 0:260].rearrange("p (a b) -> p a b", b=65)[:, :, 64:65]), reads=[("pO", oi)], writes=[("rden", oi)])
                                S.op("dve", lambda e, oi=oi: e.tensor_tensor(out=otmp[oi][:], in0=pO[oi][:, 0:260].rearrange("p (a b) -> p a b", b=65)[:, :, 0:64], in1=rden[oi][:].to_broadcast([128, 4, 64]),
                                                                             op=ALU.mult),
                                     reads=[("pO", oi), ("rden", oi)], writes=[("otmp", oi)])
                                S.op("pool", lambda e, oi=oi, p0=p0: e.tensor_tensor(
                                    out=yat[:, 4 * sp_:4 * sp_ + 4, p0:p0 + 64], in0=otmp[oi][:], in1=zs[:, 4 * sp_:4 * sp_ + 4, p0:p0 + 64], op=ALU.mult),
                                    reads=[("otmp", oi)] + [("zs", tb) for tb in range(4 * sp_, 4 * sp_ + 4)], writes=[("yat", sp_)])

                    for idx in range(len(jobs) + LA):
                        if idx < len(jobs):
                            stage1(jobs[idx])
                        if idx >= LA:
                            stage2(jobs[idx - LA])
                        if pending_casts and idx >= 16:
                            pending_casts.pop(0)()
                    while pending_casts:
                        pending_casts.pop(0)()
                    if hp < 7:
                        proj_qk(hp + 1)
                    spill_pair(hp)
                    if hp < 7:
                        proj_vz(hp + 1)
            S.barrier()

        if "C" in phases:
            with ExitStack() as pc:
                wst = [sb(f"c_wst{i}", [128, 8, 128], F32, pc) for i in range(2)]
                Wz = sb("c_Wz", [128, 8, 1024], BF16, pc)
                Wx = sb("c_Wx", [128, 8, 2048], BF16, pc)
                Wdt = sb("c_Wdt", [128, 8, 16], BF16, pc)
                cw = sb("c_cw", [128, 16, 4], F32, pc)
                cb = sb("c_cb", [128, 16], F32, pc)
                dtb = sb("c_dtb", [128, 16], F32, pc)
                a_b = sb("c_ab", [128, 16], F32, pc)
                dsk = sb("c_dsk", [128, 16], F32, pc)
                halo = sb("c_halo", [128, 16, 3], F32, pc)
                ucur = [sb(f"c_u{i}", [128, 515], F32, pc) for i in range(2)]
                xc = sb("c_xc", [128, 16, 512], BF16, pc)
                zsl = sb("c_zs", [128, 4, D], BF16, pc)
                xsT = [sb(f"c_xsT{i}", [128, D], BF16, pc) for i in range(2)]
                xdt = [sb(f"c_xdt{i}", [128, D], BF16, pc) for i in range(2)]
                xdte = [sb(f"c_xdte{i}", [128, D], BF16, pc) for i in range(2)]
                B_TM = [sb(f"c_BTM{i}", [128, 4, 128], BF16, pc) for i in range(2)]
                MT = [sb(f"c_MT{i}", [128, 16, 128], BF16, pc) for i in range(2)]
                ecs = [sb(f"c_ecs{i}", [128, 16], F32, pc) for i in range(2)]
                cdb = [sb(f"c_cdb{i}", [128, 16], F32, pc) for i in range(2)]
                dt_t = sb("c_dt", [128, 16], F32, pc)
                dA = sb("c_dA", [128, 16], F32, pc)
                R1 = sb("c_R1", [128, 8, 128], F32, pc)
                dec = sb("c_dec", [128, 16, 128], BF16, pc)
                cbm = sb("c_cbm", [128, 4, 128], BF16, pc)
                y1s = [sb(f"c_y1{i}", [128, D], F32, pc) for i in range(2)]
                y2 = sb("c_y2", [128, D], F32, pc)
                acc = [y2[:, 0:512], y2[:, 512:1024]]
                prev = sb("c_prev", [128, D], F32, pc)
                prevb = sb("c_prevb", [128, D], BF16, pc)
                gss = sb("c_gss", [128, 4], F32, pc)
                grs = sb("c_grs", [128, 4], F32, pc)
                ysb = sb("c_ysb", [128, D], BF16, pc)
                ysT = [sb(f"c_ysT{i}", [128, 8, 128], BF16, pc) for i in range(2)]
                ppx = [ps(f"c_ppx{i}", [128, 512], F32, pc) for i in range(2)]
                pseg = [ps(f"c_pseg{i}", [128, 512], F32, pc) for i in range(2)]
                pcb = ps("c_pcb", [128, 512], F32, pc)
                pyd = ps("c_pyd", [128, 512], F32, pc)
                pyo = ps("c_pyo", [128, 512], F32, pc)
                ptr = ps("c_ptr", [128, 1024], BF16, pc)

                dma(cw[:], conv_w[:, :, :], writes=["cw"])
                dma(cb[:], conv_b[:, :], writes=["cb"])
                dma(dtb[:], dt_bias.partition_broadcast(128), writes=["dtb"])
                dma(a_b[:], a_log.partition_broadcast(128), writes=["a_b"])
                dma(dsk[:], d_skip.partition_broadcast(128), writes=["dsk"])
                S.op("act", lambda e: e.activation(out=a_b[:], in_=a_b[:], func=AF.Exp), reads=["a_b"], writes=["a_b"])
                S.op("dve", lambda e: e.tensor_scalar(out=a_b[:], in0=a_b[:], scalar1=-1.0, scalar2=None, op0=ALU.mult), reads=["a_b"], writes=["a_b"])
                S.op("pool", lambda e: e.memset(halo[:], 0.0), writes=["halo"])
                S.op("pool", lambda e: e.memset(prev[:], 0.0), writes=[("prev", 0), ("prev", 1)])
                S.op("pool", lambda e: e.memset(prevb[:], 0.0), writes=[("prevb", 0), ("prevb", 1)])
                def load_wx(c4):
                    load_weight(Wx[:, :, c4 * 128:(c4 + 1) * 128], ("Wx", c4), [(w_in[:, O_XBC + c4 * 128:O_XBC + (c4 + 1) * 128], 0)], wst, "c_wst", gT, "gT", eng="dve")

                def load_wz(c4):
                    load_weight(Wz[:, :, c4 * 128:(c4 + 1) * 128], ("Wz", c4 // 4), [(w_in[:, O_ZS + c4 * 128:O_ZS + (c4 + 1) * 128], 0)], wst, "c_wst", gT, "gT", eng="dve")

                load_wx(0)
                load_wx(1)
                nu = [0]

                def slab_front(sl):
                    t0 = sl * 512
                    hk = hT_keys(t0, t0 + 512)
                    bufi = {}

                    def evac_stage(cc):
                        i = nu[0] % 2
                        nu[0] += 1
                        bufi[cc] = i
                        if sl == 0:
                            if cc + 2 < 16:
                                load_wx(cc + 2)
                            elif cc == 14:
                                load_weight(Wdt, "Wdt", [(w_in[:, O_DT:O_DT + 16], 0)], wst, "c_wst", gT, "gT", eng="dve")
                            if cc >= 8:
                                load_wz(cc - 8)
                        for k in range(8):
                            S.op("pe", lambda e, i=i, k=k, cc=cc: e.matmul(ppx[i][:], lhsT=Wx[:, k, cc * 128:(cc + 1) * 128], rhs=hT[:, k, t0:t0 + 512],
                                                                         start=(k == 0), stop=(k == 7)),
                                 reads=hk + [("Wx", cc)], writes=[("ppx", i)])
                        S.op("act", lambda e, i=i: e.copy(out=ucur[i][:, 3:515], in_=ppx[i][:]), reads=[("ppx", i)], writes=[("u", i)])
                        S.op("act", lambda e, i=i, cc=cc: e.copy(out=ucur[i][:, 0:3], in_=halo[:, cc, :]), reads=["halo"], writes=[("u", i)])
                        S.op("act", lambda e, i=i, cc=cc: e.copy(out=halo[:, cc, :], in_=ucur[i][:, 512:515]), reads=[("u", i)], writes=["halo"])

                    def conv_stage(cc):
                        i = bufi[cc]
                        S.op("dve", lambda e, i=i, cc=cc: e.tensor_scalar(out=acc[i], in0=ucur[i][:, 0:512], scalar1=cw[:, cc, 0:1], scalar2=None, op0=ALU.mult),
                             reads=[("u", i), "cw"], writes=[("y2", i)])
                        for tap in range(1, 4):
                            S.op("dve", lambda e, i=i, cc=cc, tap=tap: e.scalar_tensor_tensor(out=acc[i], in0=ucur[i][:, tap:tap + 512], scalar=cw[:, cc, tap:tap + 1],
                                                                                              in1=acc[i], op0=ALU.mult, op1=ALU.add),
                                 reads=[("u", i), "cw", ("y2", i)], writes=[("y2", i)])
                        S.op("act", lambda e, i=i, cc=cc: e.activation(out=xc[:, cc, :], in_=acc[i], func=AF.Silu, bias=cb[:, cc:cc + 1]),
                             reads=[("y2", i), "cb"], writes=[("xc", cc)])

                    evac_stage(0)
                    for cc in range(16):
                        if cc + 1 < 16:
                            evac_stage(cc + 1)
                        conv_stage(cc)
                    for ch in range(4):
                        tb = sl * 4 + ch
                        for hf in range(2):
                            i = nu[0] % 2
                            nu[0] += 1
                            for k in range(8):
                                S.op("pe", lambda e, i=i, k=k, tb=tb, hf=hf: e.matmul(ppx[i][:], lhsT=hT[:, k, tb * 128:(tb + 1) * 128], rhs=Wz[:, k, hf * 512:(hf + 1) * 512],
                                                                                     start=(k == 0), stop=(k == 7)),
                                     reads=[("hT", tb), ("Wz", hf)], writes=[("ppx", i)])
                            S.op("act", lambda e, i=i, hf=hf, ch=ch: e.activation(out=zsl[:, ch, hf * 512:(hf + 1) * 512], in_=ppx[i][:], func=AF.Silu),
                                 reads=[("ppx", i)], writes=[("zsl", ch)])

                def front(sl, ch):
                    c0 = ch * 128
                    tb = sl * 4 + ch
                    b = tb % 2
                    i = nu[0] % 2
                    nu[0] += 1
                    for k in range(8):
                        S.op("pe", lambda e, k=k: e.matmul(ppx[i][:, 0:16], lhsT=hT[:, k, tb * 128:(tb + 1) * 128], rhs=Wdt[:, k, :], start=(k == 0), stop=(k == 7)),
                             reads=[("hT", tb), "Wdt"], writes=[("ppx", i)])
                    S.op("dve", lambda e: e.tensor_tensor(out=dt_t[:], in0=ppx[i][:, 0:16], in1=dtb[:], op=ALU.add), reads=[("ppx", i), "dtb"], writes=["dt"])
                    S.op("act", lambda e: e.activation(out=dt_t[:], in_=dt_t[:], func=AF.Exp), reads=["dt"], writes=["dt"])
                    S.op("act", lambda e: e.activation(out=dt_t[:], in_=dt_t[:], func=AF.Ln, bias=1.0), reads=["dt"], writes=["dt"])
                    S.op("dve", lambda e: e.tensor_tensor(out=dA[:], in0=dt_t[:], in1=a_b[:], op=ALU.mult), reads=["dt", "a_b"], writes=["dA"])
                    yield
                    for k in range(8):
                        S.op("pe", lambda e, k=k: e.transpose(out=ptr[:, k * 128:(k + 1) * 128], in_=xc[:, k, c0:c0 + 128], identity=ident_b[:]),
                             reads=[("xc", k), "ident_b"], writes=["ptr"])
                    S.op("act", lambda e: e.copy(out=xsT[b][:], in_=ptr[:]), reads=["ptr"], writes=[("xsT", b)])
                    S.op("dve", lambda e: e.tensor_tensor(out=xdt[b][:].rearrange("p (h d) -> p h d", d=64), in0=ptr[:].rearrange("p (h d) -> p h d", d=64),
                                                          in1=dt_t[:].unsqueeze(2).to_broadcast([128, 16, 64]), op=ALU.mult),
                         reads=["ptr", "dt"], writes=[("xdt", b)])
                    yield
                    for g in range(4):
                        S.op("pe", lambda e, g=g: e.transpose(out=ptr[:, g * 128:(g + 1) * 128], in_=xc[:, 8 + g, c0:c0 + 128], identity=ident_b[:]),
                             reads=[("xc", 8 + g), "ident_b"], writes=["ptr"])
                    S.op("act", lambda e: e.copy(out=B_TM[b][:].rearrange("p g n -> p (g n)"), in_=ptr[:, 0:512]), reads=["ptr"], writes=[("B_TM", b)])
                    yield
                    for r2 in range(2):
                        S.op("dve", lambda e, r2=r2: e.tensor_tensor(out=R1[:], in0=dA[:, 8 * r2:8 * r2 + 8].unsqueeze(2).to_broadcast([128, 8, 128]),
                                                                      in1=tri_f.unsqueeze(1).to_broadcast([128, 8, 128]), op=ALU.mult),
                             reads=["dA", "cst"], writes=["R1"])
                        for q2 in range(2):
                            q4 = 2 * r2 + q2
                            j = q4 % 2
                            S.op("pe", lambda e, j=j, q2=q2: e.matmul(pseg[j][:], lhsT=sgt_f, rhs=R1[:, 4 * q2:4 * q2 + 4, :], start=True, stop=True),
                                 reads=["R1", "cst"], writes=[("pseg", j)])
                            S.op("act", lambda e, j=j, q4=q4: e.activation(out=dec[:, 4 * q4:4 * q4 + 4, :], in_=pseg[j][:].rearrange("p (h l) -> p h l", l=128), func=AF.Exp),
                                 reads=[("pseg", j)], writes=[("dec", q4)])
                    yield
                    S.op("pe", lambda e: e.matmul(pcb[:, 0:16], lhsT=tri_f, rhs=dA[:], start=True, stop=True), reads=["dA", "cst"], writes=["pcb"])
                    S.op("pe", lambda e: e.matmul(pcb[:, 16:32], lhsT=ones_f, rhs=dA[:], start=True, stop=True), reads=["dA", "cst"], writes=["pcb"])
                    S.op("act", lambda e: e.activation(out=ecs[b][:], in_=pcb[:, 0:16], func=AF.Exp), reads=["pcb"], writes=[("ecs", b)])
                    S.op("act", lambda e: e.activation(out=cdb[b][:], in_=pcb[:, 16:32], func=AF.Exp), reads=["pcb"], writes=[("cdb", b)])
                    yield
                    for g in range(4):
                        S.op("pe", lambda e, g=g: e.matmul(pcb[:, g * 128:(g + 1) * 128], lhsT=xc[:, 8 + g, c0:c0 + 128], rhs=xc[:, 12 + g, c0:c0 + 128],
                                                          start=True, stop=True),
                             reads=[("xc", 8 + g), ("xc", 12 + g)], writes=["pcb"])
                    S.op("dve", lambda e: e.tensor_tensor(out=cbm[:], in0=pcb[:].rearrange("p (g l) -> p g l", l=128),
                                                          in1=tri_f.unsqueeze(1).to_broadcast([128, 4, 128]), op=ALU.mult),
                         reads=["pcb", "cst"], writes=["cbm"])
                    for g in range(4):
                        S.op("dve", lambda e, g=g: e.tensor_tensor(out=MT[b][:, 4 * g:4 * g + 4, :], in0=dec[:, 4 * g:4 * g + 4, :],
                                                                    in1=cbm[:, g:g + 1, :].to_broadcast([128, 4, 128]), op=ALU.mult),
                             reads=[("dec", g), "cbm"], writes=[("MT", b, g)])
                    yield
                    S.op("pool", lambda e: e.tensor_tensor(out=xdte[b][:].rearrange("p (h d) -> p h d", d=64), in0=xdt[b][:].rearrange("p (h d) -> p h d", d=64),
                                                          in1=dec[:, :, 127:128].to_broadcast([128, 16, 64]), op=ALU.mult),
                         reads=[("xdt", b)] + [("dec", g) for g in range(4)], writes=[("xdte", b)])

                def backA(sl, ch):
                    c0 = ch * 128
                    tb = sl * 4 + ch
                    b = tb % 2
                    y1 = y1s[b]
                    for hf in range(2):
                        for hh in range(8):
                            h = hf * 8 + hh
                            S.op("pe", lambda e, h=h, hh=hh: e.matmul(pyd[:, hh * 64:(hh + 1) * 64], lhsT=MT[b][:, h, :], rhs=xdt[b][:, h * 64:(h + 1) * 64],
                                                                    start=True, stop=True),
                                 reads=[("MT", b, h // 4), ("xdt", b)], writes=["pyd"])
                        for gg in range(2):
                            g = hf * 2 + gg
                            S.op("pe", lambda e, g=g, gg=gg: e.matmul(pyo[:, gg * 256:(gg + 1) * 256], lhsT=xc[:, 12 + g, c0:c0 + 128],
                                                                    rhs=prevb[:, g * 256:(g + 1) * 256], start=True, stop=True),
                                 reads=[("xc", 12 + g), ("prevb", hf)], writes=["pyo"])
                        yield
                        hs = slice(hf * 512, (hf + 1) * 512)
                        S.op("dve", lambda e, hs=hs, hf=hf: e.tensor_tensor(out=y1[:, hs].rearrange("p (h d) -> p h d", d=64),
                                                                            in0=pyo[:].rearrange("p (h d) -> p h d", d=64),
                                                                            in1=ecs[b][:, hf * 8:hf * 8 + 8].unsqueeze(2).to_broadcast([128, 8, 64]), op=ALU.mult),
                             reads=["pyo", ("ecs", b)], writes=[("y1", b, hf)])
                        S.op("dve", lambda e, hs=hs, hf=hf: e.tensor_tensor(out=y1[:, hs], in0=y1[:, hs], in1=pyd[:], op=ALU.add),
                             reads=["pyd", ("y1", b, hf)], writes=[("y1", b, hf)])
                        yield
                        S.op("pool", lambda e, hs=hs, hf=hf: e.tensor_tensor(out=y2[:, hs].rearrange("p (h d) -> p h d", d=64),
                                                                             in0=xsT[b][:, hs].rearrange("p (h d) -> p h d", d=64),
                                                                             in1=dsk[:, hf * 8:hf * 8 + 8].unsqueeze(2).to_broadcast([128, 8, 64]), op=ALU.mult),
                             reads=[("xsT", b), "dsk"], writes=[("y2", hf)])
                        S.op("dve", lambda e, hs=hs, hf=hf: e.tensor_tensor(out=y1[:, hs], in0=y1[:, hs], in1=y2[:, hs], op=ALU.add),
                             reads=[("y1", b, hf), ("y2", hf)], writes=[("y1", b, hf)])
                        S.op("dve", lambda e, hs=hs, hf=hf: e.tensor_tensor(out=y1[:, hs], in0=y1[:, hs], in1=zsl[:, ch, hs], op=ALU.mult),
                             reads=[("y1", b, hf), ("zsl", ch)], writes=[("y1", b, hf)])
                        yield
                        for gg in range(2):
                            g = hf * 2 + gg
                            S.op("pe", lambda e, g=g, gg=gg: e.matmul(pyo[:, gg * 256:(gg + 1) * 256], lhsT=B_TM[b][:, g, :], rhs=xdte[b][:, g * 256:(g + 1) * 256],
                                                                    start=True, stop=True),
                                 reads=[("B_TM", b), ("xdte", b)], writes=["pyo"])
                        S.op("dve", lambda e, hs=hs, hf=hf: e.tensor_tensor(out=prev[:, hs].rearrange("p (h d) -> p h d", d=64),
                                                                            in0=prev[:, hs].rearrange("p (h d) -> p h d", d=64),
                                                                            in1=cdb[b][:, hf * 8:hf * 8 + 8].unsqueeze(2).to_broadcast([128, 8, 64]), op=ALU.mult),
                             reads=[("prev", hf), ("cdb", b)], writes=[("prev", hf)])
                        S.op("dve", lambda e, hs=hs, hf=hf: e.tensor_tensor(out=prev[:, hs], in0=prev[:, hs], in1=pyo[:], op=ALU.add),
                             reads=[("prev", hf), "pyo"], writes=[("prev", hf)])
                        S.op("act", lambda e, hs=hs, hf=hf: e.copy(out=prevb[:, hs], in_=prev[:, hs]), reads=[("prev", hf)], writes=[("prevb", hf)])

                def backB(sl, ch):
                    tb = sl * 4 + ch
                    b = tb % 2
                    y1 = y1s[b]
                    for g in range(4):
                        S.op("act", lambda e, g=g: e.activation(out=ysb[:, g * 256:(g + 1) * 256], in_=y1[:, g * 256:(g + 1) * 256], func=AF.Square,
                                                                accum_out=gss[:, g:g + 1]),
                             reads=[("y1", b, g // 2)], writes=["ysb", "gss"])
                    S.op("act", lambda e: e.activation(out=grs[:], in_=gss[:], func=AF.Ln, scale=1.0 / 256, bias=eps_t[:, 0:1]), reads=["gss", "eps"], writes=["grs"])
                    S.op("act", lambda e: e.activation(out=grs[:], in_=grs[:], func=AF.Exp, scale=-0.5), reads=["grs"], writes=["grs"])
                    for g in range(4):
                        S.op("act", lambda e, g=g: e.activation(out=ysb[:, g * 256:(g + 1) * 256], in_=y1[:, g * 256:(g + 1) * 256], func=AF.Copy, scale=grs[:, g:g + 1]),
                             reads=[("y1", b, g // 2), "grs"], writes=["ysb"])
                    yield
                    yi = tb % 2
                    for k in range(8):
                        S.op("pe", lambda e, k=k: e.transpose(out=ptr[:, k * 128:(k + 1) * 128], in_=ysb[:, k * 128:(k + 1) * 128], identity=ident_b[:]),
                             reads=["ysb", "ident_b"], writes=["ptr"])
                    S.op("act", lambda e: e.copy(out=ysT[yi][:].rearrange("p k t -> p (k t)"), in_=ptr[:]), reads=["ptr"], writes=[("ysT", yi)])
                    dma(ys_sp[:, :, tb * 128:(tb + 1) * 128].rearrange("k p t -> p k t"), ysT[yi][:], reads=[("ysT", yi)], writes=[("ys_sp", tb)])

                def run_interleaved(gens):
                    gens = [g for g in gens if g is not None]
                    while gens:
                        for g in list(gens):
                            try:
                                next(g)
                            except StopIteration:
                                gens.remove(g)

                pend_b = None
                for sl in range(NSLAB):
                    slab_front(sl)
                    run_interleaved([front(sl, 0), pend_b])
                    pend_b = None
                    for ch in range(4):
                        run_interleaved([front(sl, ch + 1) if ch < 3 else None, backA(sl, ch), pend_b])
                        pend_b = backB(sl, ch)
                run_interleaved([pend_b])
            S.barrier()

        if "D" in phases:
            with ExitStack() as pd:
                wst = [sb(f"d_wst{i}", [128, 8, 128], F32, pd) for i in range(2)]
                Wgs = sb("d_Wgs", [128, 8, D], BF16, pd)
                Wga = sb("d_Wga", [128, 8, D], BF16, pd)
                Wbs = sb("d_Wbs", [128, 8, D], BF16, pd)
                Wba = sb("d_Wba", [128, 8, D], BF16, pd)
                Wo = sb("d_Wo", [128, 8, D], BF16, pd)
                pgb = sb("d_pgb", [128, D], F32, pd)
                ysl = [sb(f"d_ysl{i}", [128, 8, 512], BF16, pd) for i in range(1)]
                yal = [sb(f"d_yal{i}", [128, 8, 512], BF16, pd) for i in range(1)]
                sgs = [sb(f"d_sgs{i}", [128, 512], F32, pd) for i in range(1)]
                sga = [sb(f"d_sga{i}", [128, 512], F32, pd) for i in range(1)]
                m1 = [sb(f"d_m1{i}", [128, 512], F32, pd) for i in range(1)]
                m2 = [sb(f"d_m2{i}", [128, 512], F32, pd) for i in range(1)]
                mT = sb("d_mT", [128, 8, 512], BF16, pd)
                xr = [sb(f"d_xr{i}", [128, D], F32, pd) for i in range(1)]
                ot = [sb(f"d_ot{i}", [128, D], F32, pd) for i in range(2)]
                oss = [sb(f"d_oss{i}", [128, 2], F32, pd) for i in range(2)]
                ors = [sb(f"d_ors{i}", [128, 1], F32, pd) for i in range(2)]
                pps = ps("d_pps", [128, 512], F32, pd)
                ppa = ps("d_ppa", [128, 512], F32, pd)
                pgs = ps("d_pgs", [128, 512], F32, pd)
                pga = ps("d_pga", [128, 512], F32, pd)
                pout = [ps(f"d_pout{i}", [128, 2, 512], F32, pd) for i in range(2)]

                def load_slab(sl):
                    t0 = sl * 512
                    dma(ysl[0][:], ys_sp[:, :, t0:t0 + 512].rearrange("k p t -> p k t"),
                        reads=[("ys_sp", tb) for tb in range(sl * 4, sl * 4 + 4)], writes=[("ysl", 0)])
                    dma(yal[0][:], ya_sp[:, :, t0:t0 + 512].rearrange("k p t -> p k t"),
                        reads=[("ya_sp", hp) for hp in range(8)], writes=[("yal", 0)])

                load_slab(0)
                dma(pgb[:], post_g.partition_broadcast(128), writes=["pgb"])
                def load_dc_weights(c4):
                    cs_ = slice(c4 * 128, (c4 + 1) * 128)
                    load_weight(Wgs[:, :, cs_], ("Wgs", c4), [(w_in[:, O_GS + c4 * 128:O_GS + (c4 + 1) * 128], 0)], wst, "d_wst", gT, "gT", eng="dve")
                    load_weight(Wga[:, :, cs_], ("Wga", c4), [(w_in[:, O_GA + c4 * 128:O_GA + (c4 + 1) * 128], 0)], wst, "d_wst", gT, "gT", eng="dve")
                    load_weight(Wbs[:, :, cs_], ("Wbs", c4), [(w_bs[:, cs_], 0)], wst, "d_wst", ngT, "ngT", eng="dve")
                    load_weight(Wba[:, :, cs_], ("Wba", c4), [(w_ba[:, cs_], 0)], wst, "d_wst", None, None, eng="dve")

                def load_wo(c4):
                    cs_ = slice(c4 * 128, (c4 + 1) * 128)
                    load_weight(Wo[:, :, cs_], ("Wo", c4), [(w_o[:, cs_], 0)], wst, "d_wst", None, None, eng="dve")

                load_dc_weights(0)
                nd = [0]
                for sl in range(NSLAB):
                    t0 = sl * 512
                    li = 0
                    hk = hT_keys(t0, t0 + 512)
                    for dc in range(8):
                        i = 0
                        dsl = slice(dc * 128, (dc + 1) * 128)
                        if sl == 0:
                            if dc < 7:
                                load_dc_weights(dc + 1)
                            load_wo(dc)
                        for k in range(8):
                            S.op("pe", lambda e, k=k, dsl=dsl, t0=t0: e.matmul(pgs[:], lhsT=Wgs[:, k, dsl], rhs=hT[:, k, t0:t0 + 512], start=(k == 0), stop=(k == 7)),
                                 reads=hk + [("Wgs", dc)], writes=["pgs"])
                        for k in range(8):
                            S.op("pe", lambda e, k=k, dsl=dsl, t0=t0: e.matmul(pga[:], lhsT=Wga[:, k, dsl], rhs=hT[:, k, t0:t0 + 512], start=(k == 0), stop=(k == 7)),
                                 reads=hk + [("Wga", dc)], writes=["pga"])
                        for k in range(8):
                            S.op("pe", lambda e, k=k, dsl=dsl, li=li: e.matmul(pps[:], lhsT=Wbs[:, k, dsl], rhs=ysl[li][:, k, :], start=(k == 0), stop=(k == 7)),
                                 reads=[("ysl", li), ("Wbs", dc)], writes=["pps"])
                        for k in range(8):
                            S.op("pe", lambda e, k=k, dsl=dsl, li=li: e.matmul(ppa[:], lhsT=Wba[:, k, dsl], rhs=yal[li][:, k, :], start=(k == 0), stop=(k == 7)),
                                 reads=[("yal", li), ("Wba", dc)], writes=["ppa"])
                        S.op("act", lambda e, i=i: e.activation(out=sgs[i][:], in_=pgs[:], func=AF.Sigmoid), reads=["pgs"], writes=[("sgs", i)])
                        S.op("act", lambda e, i=i: e.activation(out=sga[i][:], in_=pga[:], func=AF.Sigmoid), reads=["pga"], writes=[("sga", i)])
                        S.op("dve", lambda e, i=i: e.tensor_tensor(out=m1[i][:], in0=pps[:], in1=sgs[i][:], op=ALU.mult), reads=["pps", ("sgs", i)], writes=[("m1", i)])
                        S.op("dve", lambda e, i=i: e.tensor_tensor(out=m2[i][:], in0=ppa[:], in1=sga[i][:], op=ALU.mult), reads=["ppa", ("sga", i)], writes=[("m2", i)])
                        S.op("pool", lambda e, i=i, dc=dc: e.tensor_tensor(out=mT[:, dc, :], in0=m1[i][:], in1=m2[i][:], op=ALU.add),
                             reads=[("m1", i), ("m2", i)], writes=[("mT", dc)])
                    if sl < NSLAB - 1:
                        load_slab(sl + 1)
                    for ch in range(4):
                        tb = sl * 4 + ch
                        i = tb % 2
                        dma(xr[0][:], x_d[tb * 128:(tb + 1) * 128, :], writes=[("xr", 0)])
                        for hf in range(2):
                            for k in range(8):
                                S.op("pe", lambda e, i=i, k=k, hf=hf, ch=ch: e.matmul(pout[i][:, hf, :], lhsT=mT[:, k, ch * 128:(ch + 1) * 128], rhs=Wo[:, k, hf * 512:(hf + 1) * 512],
                                                                                     start=(k == 0), stop=(k == 7)),
                                     reads=[("mT", k)] + [("Wo", 4 * hf + j_) for j_ in range(4)], writes=[("pout", i)])
                        for hf in range(2):
                            S.op("act", lambda e, i=i, hf=hf: e.activation(out=ot[i][:, hf * 512:(hf + 1) * 512], in_=pout[i][:, hf, :], func=AF.Square,
                                                                           accum_out=oss[i][:, hf:hf + 1]),
                                 reads=[("pout", i)], writes=[("ot", i), ("oss", i)])
                        S.op("dve", lambda e, i=i: e.tensor_tensor(out=ors[i][:], in0=oss[i][:, 0:1], in1=oss[i][:, 1:2], op=ALU.add), reads=[("oss", i)], writes=[("ors", i)])
                        S.op("act", lambda e, i=i: e.activation(out=ors[i][:], in_=ors[i][:], func=AF.Ln, scale=1.0 / D, bias=eps_t[:, 0:1]), reads=[("ors", i), "eps"], writes=[("ors", i)])
                        S.op("act", lambda e, i=i: e.activation(out=ors[i][:], in_=ors[i][:], func=AF.Exp, scale=-0.5), reads=[("ors", i)], writes=[("ors", i)])
                        S.op("dve", lambda e, i=i: e.scalar_tensor_tensor(out=ot[i][:], in0=pout[i][:].rearrange("p a b -> p (a b)"), scalar=ors[i][:, 0:1], in1=pgb[:],
                                                                          op0=ALU.mult, op1=ALU.mult),
                             reads=[("pout", i), ("ors", i), "pgb"], writes=[("ot", i)])
                        S.op("pool", lambda e, i=i: e.tensor_tensor(out=ot[i][:], in0=ot[i][:], in1=xr[0][:], op=ALU.add), reads=[("ot", i), ("xr", 0)], writes=[("ot", i)])
                        dma(out_d[tb * 128:(tb + 1) * 128, :], ot[i][:], reads=[("ot", i)])
        S.emit(sems, dmasems)
    return nc


def _consts():
    c = np.zeros((128, 4, 128), np.float32)
    j = np.arange(128)
    c[:, 0, :] = np.eye(128, dtype=np.float32)
    c[:, 1, :] = (j[:, None] <= j[None, :]).astype(np.float32)
    c[:, 2, :] = (j[:, None] > j[None, :]).astype(np.float32)
    c[:, 3, :] = 1.0
    sel = np.zeros((16, 16, 128), np.float32)
    for h in range(16):
        sel[h, h, :] = 1.0
    return c, sel


def make_in_maps(inputs):
    c, sel = _consts()
    f = lambda a: np.ascontiguousarray(np.asarray(a, dtype=np.float32))
    shared = {
        "w_in": f(inputs["w_in"][0]), "w_bs": f(inputs["w_branch_ssm"][0]), "w_ba": f(inputs["w_branch_att"][0]),
        "w_o": f(inputs["w_out"][0]), "pre_g": f(np.asarray(inputs["pre_norm_g"][0]).reshape(8, 128).T), "post_g": f(inputs["post_norm_g"][0]),
        "norm_g": f(np.asarray(inputs["ssm_norm_g"][0]).reshape(8, 128).T), "conv_w": f(np.asarray(inputs["conv_w"][0]).reshape(4, 16, 128).transpose(2, 1, 0)), "conv_b": f(np.asarray(inputs["conv_b"][0]).reshape(16, 128).T),
        "dt_bias": f(inputs["dt_bias"][0]), "a_log": f(inputs["a_log"][0]), "d_skip": f(inputs["d_skip"][0]),
        "fgate_b": f(inputs["fgate_b"][0]), "consts": c, "sel": sel,
    }
    x = np.asarray(inputs["x"], dtype=np.float32)
    return [dict(shared, x=np.ascontiguousarray(x[b])) for b in range(x.shape[0])]


def kernel(**inputs):
    in_maps = make_in_maps(inputs)
    nc = build_nc()
    res = run_bass_kernel_spmd(nc, in_maps, core_ids=list(range(8)))
    return np.stack([np.asarray(r["out"], dtype=np.float32) for r in res.results], axis=0)
```
